# Optimizing a Trainium2 kernel written in Bass

```python
import math
import jax, jax.numpy as jnp
from jax import lax
import numpy as np

D_MODEL = 1024
BATCH = 2
SEQ = 8192
DEPTH = 4

N_MIXERS = 3
HEAD_DIM = 64
D_FF = 2816
RMS_EPS = 1e-6
BLOCK = 128
NEG_INF = -1e30
FFN_HALF = 0.5

A_GROUPS = ((128, 1), (512, 4), (2048, 16))
A_HEADS = D_MODEL // HEAD_DIM
A_IN = len(A_GROUPS) * 3 * A_HEADS * HEAD_DIM
B_HEADS = D_MODEL // (2 * HEAD_DIM)
B_IN = 3 * D_MODEL
C_HEADS = D_MODEL // HEAD_DIM
C_KV_HEADS = C_HEADS // 8
C_WINDOW = 128
C_IN = (C_HEADS + 2 * C_KV_HEADS) * HEAD_DIM

N_A = (DEPTH + 2) // 3
N_B = (DEPTH + 1) // 3
N_C = DEPTH // 3

kernel_name = "hybrid_interleaved_dilated_diff_swa_macaron"


def rms_norm(x, g):
    xf = x.astype(jnp.float32)
    y = xf * lax.rsqrt(jnp.mean(xf * xf, axis=-1, keepdims=True) + RMS_EPS)
    return (y * g.astype(jnp.float32)).astype(x.dtype)


def alibi_slopes(n_heads):
    return jnp.asarray(np.array([2.0 ** (-8.0 * (h + 1) / n_heads) for h in range(n_heads)], dtype=np.float32))


def swiglu(x, w_gate, w_up, w_down):
    return (jax.nn.silu(x @ w_gate) * (x @ w_up)) @ w_down


def diff_lambda_init(layer_idx):
    return 0.8 - 0.6 * math.exp(-0.3 * layer_idx)


def banded_attention(q, k, v, slopes, max_dist, sinks=None):
    b, l, hq, dh = q.shape
    hkv = k.shape[2]
    g = hq // hkv
    nb = -(-l // BLOCK)
    pad = nb * BLOCK - l
    padl = lambda t: jnp.pad(t, ((0, 0), (0, pad), (0, 0), (0, 0)))
    qb = padl(q).reshape(b, nb, BLOCK, hkv, g, dh)
    kb = padl(k).reshape(b, nb, BLOCK, hkv, dh)
    vb = padl(v).reshape(b, nb, BLOCK, hkv, dh)

    def with_prev(t):
        prev = jnp.pad(t, ((0, 0), (1, 0), (0, 0), (0, 0), (0, 0)))[:, :nb]
        return jnp.concatenate([prev, t], axis=2)

    kw, vw = with_prev(kb), with_prev(vb)
    scores = jnp.einsum('bnqhgd,bnkhd->bnhgqk', qb, kw).astype(jnp.float32) * (dh ** -0.5)
    kj = jnp.arange(2 * BLOCK)[None, :]
    dist = (jnp.arange(BLOCK)[:, None] + BLOCK) - kj
    key_pos = jnp.arange(nb)[:, None, None] * BLOCK - BLOCK + kj[None]
    valid = (dist >= 0) & (dist <= max_dist) & (key_pos >= 0)
    sl = slopes.astype(jnp.float32).reshape(hkv, g, 1, 1)
    scores = scores - sl * dist.astype(jnp.float32)
    scores = jnp.where(valid[None, :, None, None], scores, NEG_INF)
    m = jnp.max(scores, axis=-1)
    if sinks is not None:
        sink = sinks.astype(jnp.float32).reshape(1, 1, hkv, g, 1)
        m = jnp.maximum(m, sink)
    p = jnp.exp(scores - m[..., None])
    denom = jnp.sum(p, axis=-1)
    if sinks is not None:
        denom = denom + jnp.exp(sink - m)
    o = jnp.einsum('bnhgqk,bnkhd->bnqhgd', p / denom[..., None], vw.astype(jnp.float32))
    lse = m + jnp.log(denom)
    o = o.reshape(b, nb * BLOCK, hq, dh)[:, :l]
    lse = lse.transpose(0, 1, 4, 2, 3).reshape(b, nb * BLOCK, hq)[:, :l]
    return o, lse


def dilated_attention(h, w_in, w_out):
    b, s, _ = h.shape
    proj = (h @ w_in).reshape(b, s, len(A_GROUPS), 3, A_HEADS, HEAD_DIM)
    slopes = alibi_slopes(A_HEADS)
    outs, lses = [], []
    for gi, (window, dil) in enumerate(A_GROUPS):
        def to_classes(t):
            return t.reshape(b, s // dil, dil, A_HEADS, HEAD_DIM).transpose(0, 2, 1, 3, 4).reshape(b * dil, s // dil, A_HEADS, HEAD_DIM)
        q, k, v = (to_classes(proj[:, :, gi, c]) for c in range(3))
        o, lse = banded_attention(q, k, v, slopes * dil, window // dil)
        outs.append(o.reshape(b, dil, s // dil, A_HEADS, HEAD_DIM).transpose(0, 2, 1, 3, 4).reshape(b, s, A_HEADS, HEAD_DIM))
        lses.append(lse.reshape(b, dil, s // dil, A_HEADS).transpose(0, 2, 1, 3).reshape(b, s, A_HEADS))
    wts = jax.nn.softmax(jnp.stack(lses, axis=0), axis=0)
    o = jnp.sum(wts[..., None] * jnp.stack(outs, axis=0), axis=0)
    return o.reshape(b, s, A_HEADS * HEAD_DIM).astype(h.dtype) @ w_out


def diff_attention(h, w_in, w_out, lam, subln_g, lambda_init):
    b, s, _ = h.shape
    proj = h @ w_in
    q = proj[..., :D_MODEL].reshape(b, s, B_HEADS, 2, HEAD_DIM)
    k = proj[..., D_MODEL:2 * D_MODEL].reshape(b, s, B_HEADS, 2, HEAD_DIM)
    v = proj[..., 2 * D_MODEL:].reshape(b, s, B_HEADS, 2 * HEAD_DIM)
    lamf = lam.astype(jnp.float32)
    lam_full = jnp.exp(jnp.sum(lamf[0] * lamf[1])) - jnp.exp(jnp.sum(lamf[2] * lamf[3])) + lambda_init
    slopes = alibi_slopes(B_HEADS)[:, None, None, None]
    nb = s // BLOCK
    qb = q.reshape(b, nb, BLOCK, B_HEADS, 2, HEAD_DIM).transpose(1, 0, 2, 3, 4, 5)
    kpos = jnp.arange(s)
    scale = HEAD_DIM ** -0.5

    def block_fn(args):
        qblk, n = args
        qpos = n * BLOCK + jnp.arange(BLOCK)
        sc = jnp.einsum('bqhcd,bkhcd->bhcqk', qblk, k).astype(jnp.float32) * scale
        dist = qpos[:, None] - kpos[None, :]
        sc = jnp.where(dist >= 0, sc - slopes * dist.astype(jnp.float32), NEG_INF)
        p = jax.nn.softmax(sc, axis=-1)
        a = p[:, :, 0] - lam_full * p[:, :, 1]
        return jnp.einsum('bhqk,bkhe->bqhe', a, v.astype(jnp.float32))

    o = lax.map(block_fn, (qb, jnp.arange(nb)))
    o = o.transpose(1, 0, 2, 3, 4).reshape(b, s, B_HEADS, 2 * HEAD_DIM)
    o = rms_norm(o, subln_g) * (1.0 - lambda_init)
    return o.reshape(b, s, D_MODEL).astype(h.dtype) @ w_out


def swa_sink_attention(h, w_in, b_in, w_out, sinks):
    b, s, _ = h.shape
    proj = h @ w_in + b_in
    nq, nk = C_HEADS * HEAD_DIM, C_KV_HEADS * HEAD_DIM
    q = proj[..., :nq].reshape(b, s, C_HEADS, HEAD_DIM)
    k = proj[..., nq:nq + nk].reshape(b, s, C_KV_HEADS, HEAD_DIM)
    v = proj[..., nq + nk:].reshape(b, s, C_KV_HEADS, HEAD_DIM)
    o, _ = banded_attention(q, k, v, alibi_slopes(C_HEADS), C_WINDOW - 1, sinks)
    return o.reshape(b, s, nq).astype(h.dtype) @ w_out


def setup_inputs(seed: int = 0) -> dict:
    key = jax.random.key(seed)
    ks = jax.random.split(key, 16)
    nrm = lambda k, shape, sc: jax.random.normal(k, shape, jnp.float32) * sc
    return {
        "x": nrm(ks[0], (BATCH, SEQ, D_MODEL), 1.0),
        "norm_pre": 1.0 + nrm(ks[1], (DEPTH, 3, D_MODEL), 0.02),
        "norm_post": 1.0 + nrm(ks[2], (DEPTH, 3, D_MODEL), 0.02),
        "ffn_w_gate": nrm(ks[3], (DEPTH, 2, D_MODEL, D_FF), D_MODEL ** -0.5),
        "ffn_w_up": nrm(ks[4], (DEPTH, 2, D_MODEL, D_FF), D_MODEL ** -0.5),
        "ffn_w_down": nrm(ks[5], (DEPTH, 2, D_FF, D_MODEL), D_FF ** -0.5),
        "a_w_in": nrm(ks[6], (N_A, D_MODEL, A_IN), D_MODEL ** -0.5),
        "a_w_out": nrm(ks[7], (N_A, A_HEADS * HEAD_DIM, D_MODEL), (A_HEADS * HEAD_DIM) ** -0.5),
        "b_w_in": nrm(ks[8], (N_B, D_MODEL, B_IN), D_MODEL ** -0.5),
        "b_w_out": nrm(ks[9], (N_B, D_MODEL, D_MODEL), D_MODEL ** -0.5),
        "b_lambda": nrm(ks[10], (N_B, 4, HEAD_DIM), 0.1),
        "b_subln": 1.0 + nrm(ks[11], (N_B, 2 * HEAD_DIM), 0.02),
        "c_w_in": nrm(ks[12], (N_C, D_MODEL, C_IN), D_MODEL ** -0.5),
        "c_b_in": nrm(ks[13], (N_C, C_IN), 0.02),
        "c_w_out": nrm(ks[14], (N_C, C_HEADS * HEAD_DIM, D_MODEL), (C_HEADS * HEAD_DIM) ** -0.5),
        "c_sinks": nrm(ks[15], (N_C, C_HEADS), 0.5),
    }


def reference(x, norm_pre, norm_post, ffn_w_gate, ffn_w_up, ffn_w_down, a_w_in, a_w_out, b_w_in, b_w_out, b_lambda, b_subln, c_w_in, c_b_in, c_w_out, c_sinks):
    for i in range(DEPTH):
        f = swiglu(rms_norm(x, norm_pre[i, 0]), ffn_w_gate[i, 0], ffn_w_up[i, 0], ffn_w_down[i, 0])
        x = x + FFN_HALF * rms_norm(f, norm_post[i, 0])
        hn = rms_norm(x, norm_pre[i, 1])
        kind, j = i % N_MIXERS, i // N_MIXERS
        if kind == 0:
            m = dilated_attention(hn, a_w_in[j], a_w_out[j])
        elif kind == 1:
            m = diff_attention(hn, b_w_in[j], b_w_out[j], b_lambda[j], b_subln[j], diff_lambda_init(i))
        else:
            m = swa_sink_attention(hn, c_w_in[j], c_b_in[j], c_w_out[j], c_sinks[j])
        x = x + rms_norm(m, norm_post[i, 1])
        f = swiglu(rms_norm(x, norm_pre[i, 2]), ffn_w_gate[i, 1], ffn_w_up[i, 1], ffn_w_down[i, 1])
        x = x + FFN_HALF * rms_norm(f, norm_post[i, 2])
    return x
```

```python
import math
import numpy as np
import ml_dtypes
import concourse.bass as bass
import concourse.mybir as mybir
from concourse.bass_utils import run_bass_kernel_spmd

F32 = mybir.dt.float32
BF16 = mybir.dt.bfloat16
AF = mybir.ActivationFunctionType
ALU = mybir.AluOpType
NPBF16 = ml_dtypes.bfloat16

D = 1024
DFF = 2816
NCH = D // 128
NFC = DFF // 128
EPS = 1e-6
NCORES = 8
SEQ = 8192
BATCH = 2
TPC = 2048


class Res:
    __slots__ = ("w", "r", "name")

    def __init__(self, name=""):
        self.w = None
        self.r = []
        self.name = name


class Sched:
    ENGS = ("pe", "act", "dve", "pool", "sp")
    NDMASEM = 24

    def __init__(self, nc):
        self.nc = nc
        self.ops = {e: [] for e in self.ENGS}
        self.dma_count = [0] * self.NDMASEM
        self.dma_last = [None] * self.NDMASEM
        self.ndma = 0

    def res(self, name=""):
        return Res(name)

    def add(self, eng, fn, reads=(), writes=(), dma=False, extra_deps=()):
        deps = list(extra_deps)
        for r in reads:
            if r.w is not None:
                deps.append(r.w)
        for r in writes:
            if r.w is not None:
                deps.append(r.w)
            deps.extend(r.r)
        op = {"eng": eng, "fn": fn, "deps": deps, "signal": False, "dma": None, "idx": len(self.ops[eng])}
        if dma:
            s = self.ndma % self.NDMASEM
            self.ndma += 1
            if self.dma_last[s] is not None:
                op["deps"].append(self.dma_last[s])
            self.dma_count[s] += 1
            op["dma"] = (s, 16 * self.dma_count[s])
            self.dma_last[s] = op
        self.ops[eng].append(op)
        for r in reads:
            r.r.append(op)
        for r in writes:
            r.w = op
            r.r = []
        return op

    def emit(self):
        nc = self.nc
        for e in self.ENGS:
            for op in self.ops[e]:
                ded = []
                seen = set()
                for d in op["deps"]:
                    if id(d) in seen:
                        continue
                    seen.add(id(d))
                    if d["dma"] is None:
                        if d["eng"] == "pe" and e == "pe":
                            continue
                        d["signal"] = True
                    ded.append(d)
                op["deps"] = ded
        for e in self.ENGS:
            c = 0
            for op in self.ops[e]:
                if op["signal"]:
                    c += 1
                op["cnt"] = c
        import contextlib
        with contextlib.ExitStack() as st:
            esem = {e: st.enter_context(nc.semaphore("s_" + e)) for e in self.ENGS}
            dsem = [st.enter_context(nc.semaphore("d_%d" % i)) for i in range(self.NDMASEM)]
            block = st.enter_context(nc.Block())

            def run(e, eng):
                waited = {}
                for op in self.ops[e]:
                    need = {}
                    for d in op["deps"]:
                        if d["dma"] is not None:
                            key, val = ("d", d["dma"][0]), d["dma"][1]
                        else:
                            key, val = ("e", d["eng"]), d["cnt"]
                        if val > waited.get(key, 0) and val > need.get(key, 0):
                            need[key] = val
                    for key, val in need.items():
                        sem = dsem[key[1]] if key[0] == "d" else esem[key[1]]
                        eng.wait_ge(sem, val)
                        waited[key] = val
                    ins = op["fn"](eng)
                    if op["dma"] is not None:
                        ins.then_inc(dsem[op["dma"][0]], 16)
                    elif op["signal"]:
                        ins.then_inc(esem[e], 1)
                if e == "sp":
                    for s in range(self.NDMASEM):
                        if self.dma_count[s] > 0:
                            eng.wait_ge(dsem[s], 16 * self.dma_count[s])

            @block.tensor
            def _(eng):
                run("pe", eng)

            @block.scalar
            def _(eng):
                run("act", eng)

            @block.vector
            def _(eng):
                run("dve", eng)

            @block.gpsimd
            def _(eng):
                run("pool", eng)

            @block.sync
            def _(eng):
                run("sp", eng)


class Ctx:
    def __init__(self, name="prog"):
        self.nc = bass.Bass("TRN2", target_bir_lowering=False)
        self.s = Sched(self.nc)
        self.n = 0
        nc = self.nc
        self.ps = [nc.alloc_psum_tensor("ps%d" % i, [128, 512], F32) for i in range(8)]
        self.psr = [Res("ps%d" % i) for i in range(8)]
        self.ones_f = nc.alloc_sbuf_tensor("ones_f", [128, 128], F32)
        self.ones_r = Res("ones_f")
        self.s.add("pool", lambda e: e.memset(self.ones_f[:, :], 1.0), writes=[self.ones_r])

    def sb(self, name, shape, dt):
        return self.nc.alloc_sbuf_tensor(name, shape, dt)

    def dram_in(self, name, shape, dt=F32):
        return self.nc.dram_tensor(name, list(shape), dt, kind="ExternalInput")

    def dram_out(self, name, shape, dt=F32):
        return self.nc.dram_tensor(name, list(shape), dt, kind="ExternalOutput")


def emit_rstd(cx, src_fn, nsrc, ncols, rstd, rstd_r, src_reads, sq, sq_r, psi, inv_n=1.0 / D):
    s = cx.s
    ps, psr = cx.ps[psi], cx.psr[psi]
    for c in range(nsrc):
        s.add("act", lambda e, c=c: e.activation(out=sq[:, c % 2, :ncols], in_=src_fn(c), func=AF.Square),
              reads=src_reads(c), writes=[sq_r[c % 2]])
        s.add("pe", lambda e, c=c: e.matmul(ps[:, :ncols], cx.ones_f[:, :], sq[:, c % 2, :ncols],
                                            start=(c == 0), stop=(c == nsrc - 1)),
              reads=[sq_r[c % 2], cx.ones_r], writes=[psr])
    s.add("act", lambda e: e.activation(out=rstd[:, :ncols], in_=ps[:, :ncols], func=AF.Sqrt,
                                        scale=inv_n, bias=cx.eps_col[:, 0:1]),
          reads=[psr, cx.eps_r], writes=[rstd_r])
    s.add("dve", lambda e: e.reciprocal(out=rstd[:, :ncols], in_=rstd[:, :ncols]),
          reads=[rstd_r], writes=[rstd_r])


def build_ffn():
    cx = Ctx("ffn")
    nc, s = cx.nc, cx.s
    T = TPC
    HALF = 1024
    xT_d = cx.dram_in("xT", [D, T])
    wg_d = cx.dram_in("wg", [D, DFF])
    wu_d = cx.dram_in("wu", [D, DFF])
    wd_d = cx.dram_in("wd", [DFF, D])
    g1_d = cx.dram_in("g1", [128, NCH])
    g2_d = cx.dram_in("g2", [128, NCH])
    out_d = cx.dram_out("yT", [D, T])

    x = cx.sb("x", [128, NCH, T], F32)
    x_r = [[Res() for _ in range(T // 512)] for _ in range(NCH)]
    g1 = cx.sb("g1s", [128, NCH], F32)
    g2 = cx.sb("g2s", [128, NCH], F32)
    g1_r, g2_r = Res(), Res()
    cx.eps_col = cx.sb("eps", [128, 1], F32)
    cx.eps_r = Res()
    s.add("pool", lambda e: e.memset(cx.eps_col[:, :], EPS), writes=[cx.eps_r])
    xn = cx.sb("xn", [128, NCH, HALF], BF16)
    xn_r = [[Res() for _ in range(2)] for _ in range(NCH)]
    hT = cx.sb("hT", [128, NFC, HALF], BF16)
    hT_r = [[Res() for _ in range(2)] for _ in range(NFC)]
    fT = cx.sb("fT", [128, NCH, HALF], F32)
    fT_r = [[Res() for _ in range(2)] for _ in range(NCH)]
    sq = cx.sb("sq", [128, 2, 512], F32)
    sq_r = [Res(), Res()]
    rstd = cx.sb("rstd", [128, 512], F32)
    rstd_r = Res()
    sg = cx.sb("sg", [128, 2, 512], F32)
    sg_r = [Res(), Res()]
    tmp = cx.sb("tmp", [128, 2, 512], F32)
    tmp_r = [Res(), Res()]
    NWS = 2
    wgs = cx.sb("wgs", [128, NWS, NCH, 256], BF16)
    wus = cx.sb("wus", [128, NWS, NCH, 256], BF16)
    wgs_r = [Res() for _ in range(NWS)]
    wus_r = [Res() for _ in range(NWS)]
    NDS = 3
    wds = cx.sb("wds", [128, NDS, NFC, 128], BF16)
    wds_r = [Res() for _ in range(NDS)]

    wg_v = wg_d.ap().rearrange("(c p) f -> p c f", p=128)
    wu_v = wu_d.ap().rearrange("(c p) f -> p c f", p=128)
    wd_v = wd_d.ap().rearrange("(j p) d -> p j d", p=128)

    s.add("sp", lambda e: e.dma_start(out=g1[:, :], in_=g1_d.ap()), writes=[g1_r], dma=True)
    s.add("sp", lambda e: e.dma_start(out=g2[:, :], in_=g2_d.ap()), writes=[g2_r], dma=True)
    s.add("dve", lambda e: e.tensor_scalar(out=g2[:, :], in0=g2[:, :], scalar1=0.5, scalar2=None, op0=ALU.mult),
          reads=[g2_r], writes=[g2_r])
    for c in range(NCH):
        for t in range(T // 512):
            s.add("sp", lambda e, c=c, t=t: e.dma_start(out=x[:, c, t * 512:(t + 1) * 512],
                                                        in_=xT_d.ap()[c * 128:(c + 1) * 128, t * 512:(t + 1) * 512]),
                  writes=[x_r[c][t]], dma=True)

    wcount = [0, 0]
    for h in range(T // HALF):
        for tt in range(2):
            t = h * 2 + tt
            cs = slice(t * 512, (t + 1) * 512)
            ls = slice(tt * 512, (tt + 1) * 512)
            emit_rstd(cx, lambda c, cs=cs: x[:, c, cs], NCH, 512, rstd, rstd_r,
                      lambda c, t=t: [x_r[c][t]], sq, sq_r, 6)
            for c in range(NCH):
                s.add("dve", lambda e, c=c, cs=cs, ls=ls: e.scalar_tensor_tensor(
                    out=xn[:, c, ls], in0=x[:, c, cs], scalar=g1[:, c:c + 1], in1=rstd[:, :],
                    op0=ALU.mult, op1=ALU.mult),
                    reads=[x_r[c][t], g1_r, rstd_r], writes=[xn_r[c][tt]])
        for jj in range(NFC // 2):
            slot = wcount[0] % NWS
            wcount[0] += 1
            s.add("pool", lambda e, jj=jj, slot=slot: e.dma_start(out=wgs[:, slot, :, :], in_=wg_v[:, :, jj * 256:(jj + 1) * 256]),
                  writes=[wgs_r[slot]], dma=True)
            s.add("pool", lambda e, jj=jj, slot=slot: e.dma_start(out=wus[:, slot, :, :], in_=wu_v[:, :, jj * 256:(jj + 1) * 256]),
                  writes=[wus_r[slot]], dma=True)
            for tt in range(2):
                ls = slice(tt * 512, (tt + 1) * 512)
                for j2 in range(2):
                    j = jj * 2 + j2
                    k = cx.n % 2
                    cx.n += 1
                    pg, pu = 2 * k, 2 * k + 1
                    for c in range(NCH):
                        s.add("pe", lambda e, c=c, slot=slot, j2=j2, ls=ls, pg=pg: e.matmul(
                            cx.ps[pg][:, :], wgs[:, slot, c, j2 * 128:(j2 + 1) * 128], xn[:, c, ls],
                            start=(c == 0), stop=(c == NCH - 1)),
                            reads=[wgs_r[slot], xn_r[c][tt]], writes=[cx.psr[pg]])
                    for c in range(NCH):
                        s.add("pe", lambda e, c=c, slot=slot, j2=j2, ls=ls, pu=pu: e.matmul(
                            cx.ps[pu][:, :], wus[:, slot, c, j2 * 128:(j2 + 1) * 128], xn[:, c, ls],
                            start=(c == 0), stop=(c == NCH - 1)),
                            reads=[wus_r[slot], xn_r[c][tt]], writes=[cx.psr[pu]])
                    s.add("act", lambda e, k=k, pg=pg: e.activation(out=sg[:, k, :], in_=cx.ps[pg][:, :], func=AF.Silu),
                          reads=[cx.psr[pg]], writes=[sg_r[k]])
                    s.add("dve", lambda e, k=k, pu=pu, j=j, ls=ls: e.tensor_tensor(
                        out=hT[:, j, ls], in0=sg[:, k, :], in1=cx.ps[pu][:, :], op=ALU.mult),
                        reads=[sg_r[k], cx.psr[pu]], writes=[hT_r[j][tt]])
        for c in range(NCH):
            slot = wcount[1] % NDS
            wcount[1] += 1
            s.add("pool", lambda e, c=c, slot=slot: e.dma_start(out=wds[:, slot, :, :], in_=wd_v[:, :, c * 128:(c + 1) * 128]),
                  writes=[wds_r[slot]], dma=True)
            for tt in range(2):
                ls = slice(tt * 512, (tt + 1) * 512)
                pd = 4 + (cx.n % 2)
                cx.n += 1
                for j in range(NFC):
                    s.add("pe", lambda e, j=j, slot=slot, ls=ls, pd=pd: e.matmul(
                        cx.ps[pd][:, :], wds[:, slot, j, :], hT[:, j, ls],
                        start=(j == 0), stop=(j == NFC - 1)),
                        reads=[wds_r[slot], hT_r[j][tt]], writes=[cx.psr[pd]])
                s.add("act", lambda e, c=c, ls=ls, pd=pd: e.activation(out=fT[:, c, ls], in_=cx.ps[pd][:, :], func=AF.Copy),
                      reads=[cx.psr[pd]], writes=[fT_r[c][tt]])
        for tt in range(2):
            t = h * 2 + tt
            cs = slice(t * 512, (t + 1) * 512)
            ls = slice(tt * 512, (tt + 1) * 512)
            emit_rstd(cx, lambda c, ls=ls: fT[:, c, ls], NCH, 512, rstd, rstd_r,
                      lambda c, tt=tt: [fT_r[c][tt]], sq, sq_r, 7)
            for c in range(NCH):
                k = c % 2
                s.add("dve", lambda e, c=c, ls=ls, k=k: e.scalar_tensor_tensor(
                    out=tmp[:, k, :], in0=fT[:, c, ls], scalar=g2[:, c:c + 1], in1=rstd[:, :],
                    op0=ALU.mult, op1=ALU.mult),
                    reads=[fT_r[c][tt], g2_r, rstd_r], writes=[tmp_r[k]])
                s.add("pool", lambda e, c=c, cs=cs, k=k: e.tensor_tensor(
                    out=x[:, c, cs], in0=x[:, c, cs], in1=tmp[:, k, :], op=ALU.add),
                    reads=[tmp_r[k]], writes=[x_r[c][t]])
                s.add("sp", lambda e, c=c, cs=cs: e.dma_start(out=out_d.ap()[c * 128:(c + 1) * 128, cs], in_=x[:, c, cs]),
                      reads=[x_r[c][t]], dma=True)
    s.emit()
    return nc


def build_proj(C, has_bias):
    cx = Ctx("proj")
    nc, s = cx.nc, cx.s
    T = TPC
    NT = T // 512
    NM = C // 128
    xT_d = cx.dram_in("xT", [D, T])
    w_d = cx.dram_in("w", [D, C])
    g_d = cx.dram_in("g", [128, NCH])
    b_d = cx.dram_in("b", [128, NM]) if has_bias else None
    out_d = cx.dram_out("pT", [C, T], BF16)

    g = cx.sb("gs", [128, NCH], F32)
    g_r = Res()
    cx.eps_col = cx.sb("eps", [128, 1], F32)
    cx.eps_r = Res()
    s.add("pool", lambda e: e.memset(cx.eps_col[:, :], EPS), writes=[cx.eps_r])
    s.add("sp", lambda e: e.dma_start(out=g[:, :], in_=g_d.ap()), writes=[g_r], dma=True)
    if has_bias:
        bcol = cx.sb("bcol", [128, NM], F32)
        b_r = Res()
        s.add("sp", lambda e: e.dma_start(out=bcol[:, :], in_=b_d.ap()), writes=[b_r], dma=True)
    xs = cx.sb("xs", [128, 2, NCH, 512], F32)
    xs_r = [[Res() for _ in range(NCH)] for _ in range(2)]
    xn = cx.sb("xn", [128, NCH, T], BF16)
    xn_r = [[Res() for _ in range(NT)] for _ in range(NCH)]
    sq = cx.sb("sq", [128, 2, 512], F32)
    sq_r = [Res(), Res()]
    rstd = cx.sb("rstd", [128, 512], F32)
    rstd_r = Res()
    NWS = 3
    ws = cx.sb("ws", [128, NWS, NCH, 256], BF16)
    ws_r = [Res() for _ in range(NWS)]
    NOS = 3
    ost = cx.sb("ost", [128, NOS, T], BF16)
    ost_r = [Res() for _ in range(NOS)]
    w_v = w_d.ap().rearrange("(c p) f -> p c f", p=128)

    for t in range(NT):
        k = t % 2
        cs = slice(t * 512, (t + 1) * 512)
        for c in range(NCH):
            s.add("sp", lambda e, c=c, k=k, cs=cs: e.dma_start(out=xs[:, k, c, :], in_=xT_d.ap()[c * 128:(c + 1) * 128, cs]),
                  writes=[xs_r[k][c]], dma=True)
        emit_rstd(cx, lambda c, k=k: xs[:, k, c, :], NCH, 512, rstd, rstd_r,
                  lambda c, k=k: [xs_r[k][c]], sq, sq_r, 6 + (t % 2))
        for c in range(NCH):
            s.add("dve", lambda e, c=c, k=k, cs=cs: e.scalar_tensor_tensor(
                out=xn[:, c, cs], in0=xs[:, k, c, :], scalar=g[:, c:c + 1], in1=rstd[:, :],
                op0=ALU.mult, op1=ALU.mult),
                reads=[xs_r[k][c], g_r, rstd_r], writes=[xn_r[c][t]])
    n = 0
    for mm in range(NM // 2):
        slot = mm % NWS
        s.add("pool", lambda e, mm=mm, slot=slot: e.dma_start(out=ws[:, slot, :, :], in_=w_v[:, :, mm * 256:(mm + 1) * 256]),
              writes=[ws_r[slot]], dma=True)
        for m2 in range(2):
            m = mm * 2 + m2
            os_ = m % NOS
            for t in range(NT):
                cs = slice(t * 512, (t + 1) * 512)
                pi = n % 6
                n += 1
                for c in range(NCH):
                    s.add("pe", lambda e, c=c, slot=slot, m2=m2, cs=cs, pi=pi: e.matmul(
                        cx.ps[pi][:, :], ws[:, slot, c, m2 * 128:(m2 + 1) * 128], xn[:, c, cs],
                        start=(c == 0), stop=(c == NCH - 1)),
                        reads=[ws_r[slot], xn_r[c][t]], writes=[cx.psr[pi]])
                if has_bias:
                    s.add("act", lambda e, os_=os_, cs=cs, pi=pi, m=m: e.activation(
                        out=ost[:, os_, cs], in_=cx.ps[pi][:, :], func=AF.Identity, bias=bcol[:, m:m + 1]),
                        reads=[cx.psr[pi], b_r], writes=[ost_r[os_]])
                elif n % 2 == 0:
                    s.add("act", lambda e, os_=os_, cs=cs, pi=pi: e.activation(
                        out=ost[:, os_, cs], in_=cx.ps[pi][:, :], func=AF.Copy),
                        reads=[cx.psr[pi]], writes=[ost_r[os_]])
                else:
                    s.add("dve", lambda e, os_=os_, cs=cs, pi=pi: e.tensor_copy(out=ost[:, os_, cs], in_=cx.ps[pi][:, :]),
                          reads=[cx.psr[pi]], writes=[ost_r[os_]])
            s.add("sp", lambda e, m=m, os_=os_: e.dma_start(out=out_d.ap()[m * 128:(m + 1) * 128, :], in_=ost[:, os_, :]),
                  reads=[ost_r[os_]], dma=True)
    s.emit()
    return nc


def build_outproj():
    cx = Ctx("outproj")
    nc, s = cx.nc, cx.s
    T = TPC
    NT = T // 512
    xT_d = cx.dram_in("xT", [D, T])
    oT_d = cx.dram_in("oT", [D, T], BF16)
    w_d = cx.dram_in("w", [D, D])
    g_d = cx.dram_in("g", [128, NCH])
    out_d = cx.dram_out("yT", [D, T])
    g = cx.sb("gs", [128, NCH], F32)
    g_r = Res()
    cx.eps_col = cx.sb("eps", [128, 1], F32)
    cx.eps_r = Res()
    s.add("pool", lambda e: e.memset(cx.eps_col[:, :], EPS), writes=[cx.eps_r])
    s.add("sp", lambda e: e.dma_start(out=g[:, :], in_=g_d.ap()), writes=[g_r], dma=True)
    w = cx.sb("wsb", [128, NCH, D], BF16)
    w_r = Res()
    s.add("pool", lambda e: e.dma_start(out=w[:, :, :], in_=w_d.ap().rearrange("(c p) f -> p c f", p=128)),
          writes=[w_r], dma=True)
    o = cx.sb("osb", [128, NCH, T], BF16)
    o_r = [Res() for _ in range(NCH)]
    for c in range(NCH):
        s.add("sp", lambda e, c=c: e.dma_start(out=o[:, c, :], in_=oT_d.ap()[c * 128:(c + 1) * 128, :]),
              writes=[o_r[c]], dma=True)
    x = cx.sb("x", [128, NCH, T], F32)
    x_r = [[Res() for _ in range(NT)] for _ in range(NCH)]
    for c in range(NCH):
        for t in range(NT):
            s.add("sp", lambda e, c=c, t=t: e.dma_start(out=x[:, c, t * 512:(t + 1) * 512],
                                                        in_=xT_d.ap()[c * 128:(c + 1) * 128, t * 512:(t + 1) * 512]),
                  writes=[x_r[c][t]], dma=True)
    fT = cx.sb("fT", [128, NCH, 512], F32)
    fT_r = [Res() for _ in range(NCH)]
    sq = cx.sb("sq", [128, 2, 512], F32)
    sq_r = [Res(), Res()]
    rstd = cx.sb("rstd", [128, 512], F32)
    rstd_r = Res()
    tmp = cx.sb("tmp", [128, 2, 512], F32)
    tmp_r = [Res(), Res()]
    n = 0
    for t in range(NT):
        cs = slice(t * 512, (t + 1) * 512)
        for c in range(NCH):
            pi = n % 4
            n += 1
            for k in range(NCH):
                s.add("pe", lambda e, c=c, k=k, cs=cs, pi=pi: e.matmul(
                    cx.ps[pi][:, :], w[:, k, c * 128:(c + 1) * 128], o[:, k, cs],
                    start=(k == 0), stop=(k == NCH - 1)),
                    reads=[w_r, o_r[k]], writes=[cx.psr[pi]])
            s.add("act", lambda e, c=c, pi=pi: e.activation(out=fT[:, c, :], in_=cx.ps[pi][:, :], func=AF.Copy),
                  reads=[cx.psr[pi]], writes=[fT_r[c]])
        emit_rstd(cx, lambda c: fT[:, c, :], NCH, 512, rstd, rstd_r, lambda c: [fT_r[c]], sq, sq_r, 6 + (t % 2))
        for c in range(NCH):
            k = c % 2
            s.add("dve", lambda e, c=c, k=k: e.scalar_tensor_tensor(
                out=tmp[:, k, :], in0=fT[:, c, :], scalar=g[:, c:c + 1], in1=rstd[:, :],
                op0=ALU.mult, op1=ALU.mult),
                reads=[fT_r[c], g_r, rstd_r], writes=[tmp_r[k]])
            s.add("pool", lambda e, c=c, cs=cs, k=k: e.tensor_tensor(
                out=x[:, c, cs], in0=x[:, c, cs], in1=tmp[:, k, :], op=ALU.add),
                reads=[tmp_r[k]], writes=[x_r[c][t]])
            s.add("sp", lambda e, c=c, cs=cs: e.dma_start(out=out_d.ap()[c * 128:(c + 1) * 128, cs], in_=x[:, c, cs]),
                  reads=[x_r[c][t]], dma=True)
    s.emit()
    return nc


def build_banded(dils, has_sink):
    cx = Ctx("banded")
    nc, s = cx.nc, cx.s
    G = len(dils)
    NB = SEQ // 128
    q_d = cx.dram_in("q_in", [2, G, 128, SEQ], BF16)
    k_d = cx.dram_in("k_in", [2, G, 128, SEQ], BF16)
    v_d = cx.dram_in("v_in", [2, G, 128, NB, 128], BF16)
    dn_d = cx.dram_in("dneg", [128, 256])
    cst_d = cx.dram_in("cst", [128, 2 * G * 2])
    sk_d = cx.dram_in("sink", [128, 2]) if has_sink else None
    out_d = cx.dram_out("oT", [2, 128, SEQ], BF16)

    dneg = cx.sb("dnegs", [128, 256], F32)
    dneg_r = Res()
    s.add("sp", lambda e: e.dma_start(out=dneg[:, :], in_=dn_d.ap()), writes=[dneg_r], dma=True)
    cst = cx.sb("csts", [128, 2 * G * 2], F32)
    cst_r = Res()
    s.add("sp", lambda e: e.dma_start(out=cst[:, :], in_=cst_d.ap()), writes=[cst_r], dma=True)
    if has_sink:
        sk = cx.sb("sks", [128, 2], F32)
        sk_r = Res()
        s.add("sp", lambda e: e.dma_start(out=sk[:, :], in_=sk_d.ap()), writes=[sk_r], dma=True)
        s.add("act", lambda e: e.activation(out=sk[:, :], in_=sk[:, :], func=AF.Exp), reads=[sk_r], writes=[sk_r])
    ones_b = cx.sb("ones_b", [128, 64], BF16)
    ones_b_r = Res()
    s.add("pool", lambda e: e.memset(ones_b[:, :], 1.0), writes=[ones_b_r])

    NQS = 6
    qs = cx.sb("qs", [128, NQS, 2048], BF16)
    ks = cx.sb("ks", [128, NQS, 2048], BF16)
    vs = cx.sb("vs", [128, NQS, 16, 128], BF16)
    qs_r = [Res() for _ in range(NQS)]
    ks_r = [Res() for _ in range(NQS)]
    vs_r = [Res() for _ in range(NQS)]
    accn = cx.sb("accn", [128, SEQ], F32)
    accd = cx.sb("accd", [128, SEQ], F32)
    acc_r = [Res() for _ in range(NB)]
    NP = 4
    pb = cx.sb("pb", [128, NP, 2, 256], BF16)
    pb_r = [[Res(), Res()] for _ in range(NP)]
    NSB = 4
    sbb = cx.sb("sbb", [128, NSB, 256], F32)
    sbb_r = [Res() for _ in range(NSB)]
    ost = cx.sb("ost", [128, 2, 2048], BF16)
    ost_r = [Res(), Res()]

    it = 0
    nS = 0
    nPV = 0
    for pr in range(2):
        for g in range(G):
            d = dils[g]
            nbc = NB // d
            slot_of = lambda Q, it=it: (it * 4 + Q) % NQS
            for Q in range(4):
                sl = slot_of(Q)
                s.add("sp", lambda e, pr=pr, g=g, Q=Q, sl=sl: e.dma_start(
                    out=qs[:, sl, :], in_=q_d.ap()[pr, g, :, Q * 2048:(Q + 1) * 2048]), writes=[qs_r[sl]], dma=True)
                s.add("sp", lambda e, pr=pr, g=g, Q=Q, sl=sl: e.dma_start(
                    out=ks[:, sl, :], in_=k_d.ap()[pr, g, :, Q * 2048:(Q + 1) * 2048]), writes=[ks_r[sl]], dma=True)
                s.add("sp", lambda e, pr=pr, g=g, Q=Q, sl=sl: e.dma_start(
                    out=vs[:, sl, :, :], in_=v_d.ap()[pr, g, :, Q * 16:(Q + 1) * 16, :]), writes=[vs_r[sl]], dma=True)

            def emit_S(B, pr=pr, g=g, d=d, nbc=nbc, slot_of=slot_of):
                nonlocal nS
                b = B % nbc
                last = (b == nbc - 1)
                n = 128 if last else 256
                Q, lb = B // 16, B % 16
                sl = slot_of(Q)
                for e_ in range(2):
                    pi = nS % 4
                    si = nS % NSB
                    nS += 1
                    pslot = B % NP
                    rows = slice(64 * e_, 64 * e_ + 64)
                    rd = [ks_r[sl], qs_r[sl]]
                    s.add("pe", lambda e, pi=pi, sl=sl, lb=lb, rows=rows: e.matmul(
                        cx.ps[pi][:, 0:128], ks[rows, sl, lb * 128:(lb + 1) * 128], qs[rows, sl, lb * 128:(lb + 1) * 128],
                        start=True, stop=True), reads=rd, writes=[cx.psr[pi]])
                    if not last:
                        Q2, lb2 = (B + 1) // 16, (B + 1) % 16
                        sl2 = slot_of(Q2)
                        s.add("pe", lambda e, pi=pi, sl=sl, sl2=sl2, lb=lb, lb2=lb2, rows=rows: e.matmul(
                            cx.ps[pi][:, 128:256], ks[rows, sl, lb * 128:(lb + 1) * 128], qs[rows, sl2, lb2 * 128:(lb2 + 1) * 128],
                            start=True, stop=True), reads=[ks_r[sl], qs_r[sl2]], writes=[cx.psr[pi]])
                    ci = (pr * G + g) * 2 + e_
                    s.add("dve", lambda e, si=si, pi=pi, n=n, ci=ci: e.scalar_tensor_tensor(
                        out=sbb[:, si, 0:n], in0=dneg[:, 0:n], scalar=cst[:, ci:ci + 1], in1=cx.ps[pi][:, 0:n],
                        op0=ALU.mult, op1=ALU.add), reads=[dneg_r, cst_r, cx.psr[pi]], writes=[sbb_r[si]])
                    s.add("act", lambda e, si=si, pslot=pslot, e_=e_, n=n: e.activation(
                        out=pb[:, pslot, e_, 0:n], in_=sbb[:, si, 0:n], func=AF.Exp, scale=0.125),
                        reads=[sbb_r[si]], writes=[pb_r[pslot][e_]])

            def emit_PV(B, pr=pr, g=g, d=d, nbc=nbc, slot_of=slot_of):
                nonlocal nPV
                b = B % nbc
                r = B // nbc
                pn, pd_ = 4 + (nPV % 2), 6 + (nPV % 2)
                nPV += 1
                srcs = []
                if b > 0:
                    srcs.append((B - 1, slice(128, 256)))
                srcs.append((B, slice(0, 128)))
                for e_ in range(2):
                    orow = slice(64 * e_, 64 * e_ + 64)
                    for i, (kb, cols) in enumerate(srcs):
                        Qk, lbk = kb // 16, kb % 16
                        slk = slot_of(Qk)
                        s.add("pe", lambda e, pn=pn, orow=orow, slk=slk, lbk=lbk, kb=kb, e_=e_, cols=cols, i=i: e.matmul(
                            cx.ps[pn][orow, 0:128], vs[:, slk, lbk, 64 * e_:64 * e_ + 64], pb[:, kb % NP, e_, cols],
                            start=(i == 0), stop=(i == len(srcs) - 1)),
                            reads=[vs_r[slk], pb_r[kb % NP][e_]], writes=[cx.psr[pn]])
                    for i, (kb, cols) in enumerate(srcs):
                        s.add("pe", lambda e, pd_=pd_, orow=orow, kb=kb, e_=e_, cols=cols, i=i: e.matmul(
                            cx.ps[pd_][orow, 0:128], ones_b[:, :], pb[:, kb % NP, e_, cols],
                            start=(i == 0), stop=(i == len(srcs) - 1)),
                            reads=[ones_b_r, pb_r[kb % NP][e_]], writes=[cx.psr[pd_]])
                off = b * 128 * d + r
                view = slice(off, off + 127 * d + 1, d) if d > 1 else slice(off, off + 128)
                ar = [acc_r[j] for j in range(b * d, (b + 1) * d)]
                if g == 0:
                    s.add("act", lambda e, pn=pn, view=view: e.activation(out=accn[:, view], in_=cx.ps[pn][:, 0:128], func=AF.Copy),
                          reads=[cx.psr[pn]], writes=ar)
                    if has_sink:
                        s.add("dve", lambda e, pd_=pd_, view=view, pr=pr: e.tensor_scalar(
                            out=accd[:, view], in0=cx.ps[pd_][:, 0:128], scalar1=sk[:, pr:pr + 1], scalar2=None, op0=ALU.add),
                            reads=[cx.psr[pd_], sk_r], writes=ar)
                    else:
                        s.add("dve", lambda e, pd_=pd_, view=view: e.tensor_copy(out=accd[:, view], in_=cx.ps[pd_][:, 0:128]),
                              reads=[cx.psr[pd_]], writes=ar)
                else:
                    s.add("dve", lambda e, pn=pn, view=view: e.tensor_tensor(
                        out=accn[:, view], in0=accn[:, view], in1=cx.ps[pn][:, 0:128], op=ALU.add),
                        reads=[cx.psr[pn]], writes=ar)
                    s.add("dve", lambda e, pd_=pd_, view=view: e.tensor_tensor(
                        out=accd[:, view], in0=accd[:, view], in1=cx.ps[pd_][:, 0:128], op=ALU.add),
                        reads=[cx.psr[pd_]], writes=ar)

            emit_S(0)
            for B in range(NB):
                if B + 1 < NB:
                    emit_S(B + 1)
                emit_PV(B)
            it += 1
        for Q in range(4):
            cs = slice(Q * 2048, (Q + 1) * 2048)
            ar = [acc_r[j] for j in range(Q * 16, (Q + 1) * 16)]
            k = Q % 2
            s.add("dve", lambda e, cs=cs: e.reciprocal(out=accd[:, cs], in_=accd[:, cs]), reads=ar, writes=ar)
            s.add("pool", lambda e, cs=cs, k=k: e.tensor_tensor(out=ost[:, k, :], in0=accn[:, cs], in1=accd[:, cs], op=ALU.mult),
                  reads=ar, writes=[ost_r[k]])
            s.add("pool", lambda e, pr=pr, cs=cs, k=k: e.dma_start(out=out_d.ap()[pr, :, cs], in_=ost[:, k, :]),
                  reads=[ost_r[k]], dma=True)
    s.emit()
    return nc


def alibi_slopes(n):
    return np.array([2.0 ** (-8.0 * (h + 1) / n) for h in range(n)], dtype=np.float64)


MASKV = -1.0e30


def dneg_table(max_dist):
    k = np.arange(128)[:, None]
    q = np.arange(128)[None, :]
    cur = np.where(q - k >= 0, -(q - k).astype(np.float64), MASKV)
    dn = q + 128 - k
    nxt = np.where(dn <= max_dist, -dn.astype(np.float64), MASKV)
    return np.ascontiguousarray(np.concatenate([cur, nxt], axis=1).astype(np.float32))


def to_classes(a, d):
    if d == 1:
        return a
    sh = a.shape
    return np.ascontiguousarray(a.reshape(sh[:-1] + (SEQ // d, d)).swapaxes(-1, -2)).reshape(sh)


def v_layout(vT):
    return np.ascontiguousarray(vT.reshape(128, SEQ // 128, 128).transpose(2, 1, 0))


def build_diff(lambda_init):
    cx = Ctx("diff")
    nc, s = cx.nc, cx.s
    NB = SEQ // 128
    NQT = SEQ // 512
    q_d = cx.dram_in("q_in", [2, 128, SEQ], BF16)
    k_d = cx.dram_in("k_in", [2, 128, SEQ], BF16)
    v_d = cx.dram_in("v_in", [2, 128, NB, 128], BF16)
    mt_d = cx.dram_in("mtab", [128, 5, 512])
    cst_d = cx.dram_in("cst", [128, 2 + 128])
    lam_d = cx.dram_in("lam", [128, 4, 64])
    sub_d = cx.dram_in("sub", [128, 1])
    out_d = cx.dram_out("oT", [2, 128, SEQ], BF16)

    mtab = cx.sb("mtabs", [128, 5, 512], F32)
    mtab_r = Res()
    s.add("sp", lambda e: e.dma_start(out=mtab[:, :, :], in_=mt_d.ap()), writes=[mtab_r], dma=True)
    cst = cx.sb("csts", [128, 2 + 128], F32)
    cst_r = Res()
    s.add("sp", lambda e: e.dma_start(out=cst[:, :], in_=cst_d.ap()), writes=[cst_r], dma=True)
    lam = cx.sb("lams", [128, 4, 64], F32)
    lam_r = Res()
    s.add("sp", lambda e: e.dma_start(out=lam[:, :, :], in_=lam_d.ap()), writes=[lam_r], dma=True)
    sub = cx.sb("subs", [128, 1], F32)
    sub_r = Res()
    s.add("sp", lambda e: e.dma_start(out=sub[:, :], in_=sub_d.ap()), writes=[sub_r], dma=True)
    cx.eps_col = cx.sb("eps", [128, 1], F32)
    cx.eps_r = Res()
    s.add("pool", lambda e: e.memset(cx.eps_col[:, :], EPS), writes=[cx.eps_r])
    ones_b = cx.sb("ones_b", [128, 128], BF16)
    ones_b_r = Res()
    s.add("pool", lambda e: e.memset(ones_b[:, :], 1.0), writes=[ones_b_r])
    lp = cx.sb("lp", [128, 2, 64], F32)
    lsum = cx.sb("lsum", [128, 2], F32)
    neglam = cx.sb("neglam", [128, 1], F32)
    lp_r, lsum_r, neglam_r = Res(), Res(), Res()
    for i in range(2):
        s.add("dve", lambda e, i=i: e.tensor_tensor(out=lp[:, i, :], in0=lam[:, 2 * i, :], in1=lam[:, 2 * i + 1, :], op=ALU.mult),
              reads=[lam_r], writes=[lp_r])
        s.add("dve", lambda e, i=i: e.reduce_sum(out=lsum[:, i:i + 1], in_=lp[:, i, :], axis=mybir.AxisListType.X),
              reads=[lp_r], writes=[lsum_r])
    s.add("act", lambda e: e.activation(out=lsum[:, :], in_=lsum[:, :], func=AF.Exp), reads=[lsum_r], writes=[lsum_r])
    s.add("dve", lambda e: e.tensor_tensor(out=neglam[:, :], in0=lsum[:, 1:2], in1=lsum[:, 0:1], op=ALU.subtract),
          reads=[lsum_r], writes=[neglam_r])
    s.add("dve", lambda e: e.tensor_scalar(out=neglam[:, :], in0=neglam[:, :], scalar1=-float(lambda_init), scalar2=None, op0=ALU.add),
          reads=[neglam_r], writes=[neglam_r])

    qs = cx.sb("qs", [128, 2, SEQ], BF16)
    ks = cx.sb("ks", [128, 2, SEQ], BF16)
    vs = cx.sb("vs", [128, 2, NB, 128], BF16)
    qs_r = [[Res() for _ in range(4)] for _ in range(2)]
    ks_r = [[Res() for _ in range(4)] for _ in range(2)]
    vs_r = [[Res() for _ in range(4)] for _ in range(2)]
    for i in range(2):
        for Q in range(4):
            s.add("sp", lambda e, i=i, Q=Q: e.dma_start(out=qs[:, i, Q * 2048:(Q + 1) * 2048], in_=q_d.ap()[i, :, Q * 2048:(Q + 1) * 2048]),
                  writes=[qs_r[i][Q]], dma=True)
            s.add("sp", lambda e, i=i, Q=Q: e.dma_start(out=ks[:, i, Q * 2048:(Q + 1) * 2048], in_=k_d.ap()[i, :, Q * 2048:(Q + 1) * 2048]),
                  writes=[ks_r[i][Q]], dma=True)
            s.add("sp", lambda e, i=i, Q=Q: e.dma_start(out=vs[:, i, Q * 16:(Q + 1) * 16, :], in_=v_d.ap()[i, :, Q * 16:(Q + 1) * 16, :]),
                  writes=[vs_r[i][Q]], dma=True)
    NP = 6
    pb = cx.sb("pb", [128, NP, 512], BF16)
    pb_r = [Res() for _ in range(NP)]
    NSB = 4
    sbb = cx.sb("sbb", [128, NSB, 512], F32)
    sbb_r = [Res() for _ in range(NSB)]
    rd = cx.sb("rd", [128, 2, 512], F32)
    rd_r = [Res(), Res()]
    tt_ = cx.sb("tt", [128, 2, 512], F32)
    tt_r = [Res(), Res()]
    ob = cx.sb("ob", [128, 512], F32)
    ob_r = Res()
    sq = cx.sb("sq", [128, 2, 512], F32)
    sq_r = [Res(), Res()]
    rstd = cx.sb("rstd", [128, 512], F32)
    rstd_r = Res()
    ost = cx.sb("ost", [128, 2, 512], BF16)
    ost_r = [Res(), Res()]

    nS = 0
    for i in range(2):
        for qt in range(NQT):
            Qq = qt // 4
            qc0 = qt * 512
            kblocks = [(kb, None) for kb in range(4 * qt)] + [(4 * qt + j, j) for j in range(4)]
            nk = len(kblocks)

            def emit_S(idx, i=i, qt=qt, Qq=Qq, qc0=qc0, kblocks=kblocks):
                nonlocal nS
                kb, dj = kblocks[idx]
                Qk = kb // 16
                c0 = 0 if dj is None else 128 * dj
                mi = 0 if dj is None else 1 + dj
                nrel = 0 if dj is None else 0
                if dj is None:
                    nrel = 4 * qt - kb
                res = []
                for c in range(2):
                    pi = nS % 4
                    si = nS % NSB
                    pslot = nS % NP
                    nS += 1
                    rows = slice(64 * c, 64 * c + 64)
                    s.add("pe", lambda e, pi=pi, rows=rows, kb=kb, c0=c0: e.matmul(
                        cx.ps[pi][:, c0:512], ks[rows, i, kb * 128:(kb + 1) * 128], qs[rows, i, qc0 + c0:qc0 + 512],
                        start=True, stop=True), reads=[ks_r[i][Qk], qs_r[i][Qq]], writes=[cx.psr[pi]])
                    s.add("dve", lambda e, si=si, pi=pi, mi=mi, c0=c0: e.scalar_tensor_tensor(
                        out=sbb[:, si, c0:512], in0=mtab[:, mi, c0:512], scalar=cst[:, i:i + 1], in1=cx.ps[pi][:, c0:512],
                        op0=ALU.mult, op1=ALU.add), reads=[mtab_r, cst_r, cx.psr[pi]], writes=[sbb_r[si]])
                    bc = 2 + i * 64 + nrel
                    s.add("act", lambda e, si=si, pslot=pslot, c0=c0, bc=bc: e.activation(
                        out=pb[:, pslot, c0:512], in_=sbb[:, si, c0:512], func=AF.Exp, scale=0.125, bias=cst[:, bc:bc + 1]),
                        reads=[sbb_r[si], cst_r], writes=[pb_r[pslot]])
                    res.append((pslot, c0))
                return res

            def emit_PV(idx, pinfo, i=i, kblocks=kblocks, nk=nk):
                kb, dj = kblocks[idx]
                Qk = kb // 16
                for c in range(2):
                    pslot, c0 = pinfo[c]
                    s.add("pe", lambda e, c=c, kb=kb, pslot=pslot, c0=c0, idx=idx: e.matmul(
                        cx.ps[4 + c][:, c0:512], vs[:, i, kb, :], pb[:, pslot, c0:512],
                        start=(idx == 0), stop=(idx == nk - 1)),
                        reads=[vs_r[i][Qk], pb_r[pslot]], writes=[cx.psr[4 + c]])
                    s.add("pe", lambda e, c=c, pslot=pslot, c0=c0, idx=idx: e.matmul(
                        cx.ps[6 + c][:, c0:512], ones_b[:, :], pb[:, pslot, c0:512],
                        start=(idx == 0), stop=(idx == nk - 1)),
                        reads=[ones_b_r, pb_r[pslot]], writes=[cx.psr[6 + c]])

            pin = emit_S(0)
            for idx in range(nk):
                nxt = emit_S(idx + 1) if idx + 1 < nk else None
                emit_PV(idx, pin)
                pin = nxt
            for c in range(2):
                s.add("dve", lambda e, c=c: e.reciprocal(out=rd[:, c, :], in_=cx.ps[6 + c][:, :]),
                      reads=[cx.psr[6 + c]], writes=[rd_r[c]])
                s.add("dve", lambda e, c=c: e.tensor_tensor(out=tt_[:, c, :], in0=cx.ps[4 + c][:, :], in1=rd[:, c, :], op=ALU.mult),
                      reads=[cx.psr[4 + c], rd_r[c]], writes=[tt_r[c]])
            s.add("dve", lambda e: e.scalar_tensor_tensor(out=ob[:, :], in0=tt_[:, 1, :], scalar=neglam[:, 0:1], in1=tt_[:, 0, :],
                                                          op0=ALU.mult, op1=ALU.add),
                  reads=[tt_r[0], tt_r[1], neglam_r], writes=[ob_r])
            emit_rstd(cx, lambda c: ob[:, :], 1, 512, rstd, rstd_r, lambda c: [ob_r], sq, sq_r, (nS % 4), inv_n=1.0 / 128)
            s.add("dve", lambda e: e.scalar_tensor_tensor(out=ob[:, :], in0=ob[:, :], scalar=sub[:, 0:1], in1=rstd[:, :],
                                                          op0=ALU.mult, op1=ALU.mult),
                  reads=[ob_r, sub_r, rstd_r], writes=[ob_r])
            k_ = qt % 2
            s.add("act", lambda e, k_=k_: e.activation(out=ost[:, k_, :], in_=ob[:, :], func=AF.Copy, scale=float(1.0 - lambda_init)),
                  reads=[ob_r], writes=[ost_r[k_]])
            s.add("pool", lambda e, i=i, qc0=qc0, k_=k_: e.dma_start(out=out_d.ap()[i, :, qc0:qc0 + 512], in_=ost[:, k_, :]),
                  reads=[ost_r[k_]], dma=True)
    s.emit()
    return nc


def diff_mtab():
    k = np.arange(128)[:, None].astype(np.float64)
    q = np.arange(512)[None, :].astype(np.float64)
    tabs = [-(q - k)]
    for j in range(4):
        val = q - 128 * j - k
        tabs.append(np.where(val >= 0, -val, MASKV))
    return np.ascontiguousarray(np.stack(tabs, axis=1).astype(np.float32))


_PROGS = {}


def _prog(key, builder):
    if key not in _PROGS:
        _PROGS[key] = builder()
    return _PROGS[key]


def _run(nc, in_maps):
    res = run_bass_kernel_spmd(nc, in_maps, core_ids=list(range(NCORES)))
    return res.results


def _lay(g):
    return np.ascontiguousarray(np.asarray(g, dtype=np.float32).reshape(-1, 128).T)


def _f32(a):
    return np.ascontiguousarray(np.asarray(a, dtype=np.float32))


def _ffn(xTs, wg, wu, wd, g1, g2):
    nc = _prog("ffn", build_ffn)
    wg, wu, wd = _f32(wg), _f32(wu), _f32(wd)
    g1, g2 = _lay(g1), _lay(g2)
    res = _run(nc, [dict(xT=xTs[c], wg=wg, wu=wu, wd=wd, g1=g1, g2=g2) for c in range(NCORES)])
    return [np.asarray(r["yT"]) for r in res]


def _proj(xTs, w, g, b=None):
    C = w.shape[1]
    nc = _prog(("proj", C, b is not None), lambda: build_proj(C, b is not None))
    w, g = _f32(w), _lay(g)
    maps = []
    for c in range(NCORES):
        m = dict(xT=xTs[c], w=w, g=g)
        if b is not None:
            m["b"] = _lay(b)
        maps.append(m)
    res = _run(nc, maps)
    pT = [np.asarray(r["pT"]) for r in res]
    return [np.concatenate(pT[4 * b_:4 * b_ + 4], axis=1) for b_ in range(BATCH)]


def _outproj(xTs, oT_full, w, g):
    nc = _prog("outproj", build_outproj)
    w, g = _f32(w), _lay(g)
    maps = []
    for c in range(NCORES):
        b_, ch = c // 4, c % 4
        maps.append(dict(xT=xTs[c], oT=np.ascontiguousarray(oT_full[b_][:, ch * TPC:(ch + 1) * TPC]), w=w, g=g))
    res = _run(nc, maps)
    return [np.asarray(r["yT"]) for r in res]


A_DILS = (1, 4, 16)


def _attn_A(pT):
    nc = _prog("bandA", lambda: build_banded(A_DILS, False))
    G = 3
    slopes = alibi_slopes(16)
    dneg = dneg_table(128)
    maps = []
    for c in range(NCORES):
        b_, hq = c // 4, c % 4
        P = pT[b_]
        q_in = np.empty((2, G, 128, SEQ), NPBF16)
        k_in = np.empty((2, G, 128, SEQ), NPBF16)
        v_in = np.empty((2, G, 128, SEQ // 128, 128), NPBF16)
        cst = np.zeros((128, 2 * G * 2), np.float32)
        for pr in range(2):
            h0 = 4 * hq + 2 * pr
            for g in range(G):
                d = A_DILS[g]
                base = lambda cc: ((g * 3 + cc) * 16 + h0) * 64
                q_in[pr, g] = to_classes(P[base(0):base(0) + 128], d)
                k_in[pr, g] = to_classes(P[base(1):base(1) + 128], d)
                v_in[pr, g] = v_layout(to_classes(P[base(2):base(2) + 128], d))
                for e in range(2):
                    cst[:, (pr * G + g) * 2 + e] = 8.0 * slopes[h0 + e] * d
        maps.append(dict(q_in=q_in, k_in=k_in, v_in=v_in, dneg=dneg, cst=cst))
    res = _run(nc, maps)
    oT = [np.empty((D, SEQ), NPBF16) for _ in range(BATCH)]
    for c in range(NCORES):
        b_, hq = c // 4, c % 4
        oT[b_][hq * 256:(hq + 1) * 256] = np.asarray(res[c]["oT"]).reshape(256, SEQ)
    return oT


def _attn_C(pT, sinks):
    nc = _prog("bandC", lambda: build_banded((1,), True))
    slopes = alibi_slopes(16)
    dneg = dneg_table(127)
    sinks = np.asarray(sinks, dtype=np.float32)
    maps = []
    for c in range(NCORES):
        b_, hq = c // 4, c % 4
        P = pT[b_]
        q_in = np.empty((2, 1, 128, SEQ), NPBF16)
        k_in = np.empty((2, 1, 128, SEQ), NPBF16)
        v_in = np.empty((2, 1, 128, SEQ // 128, 128), NPBF16)
        cst = np.zeros((128, 4), np.float32)
        sk = np.zeros((128, 2), np.float32)
        for pr in range(2):
            h0 = 4 * hq + 2 * pr
            kv = h0 // 8
            q_in[pr, 0] = P[h0 * 64:h0 * 64 + 128]
            kk = P[1024 + kv * 64:1024 + kv * 64 + 64]
            vv = P[1152 + kv * 64:1152 + kv * 64 + 64]
            k_in[pr, 0] = np.concatenate([kk, kk], axis=0)
            v_in[pr, 0] = v_layout(np.concatenate([vv, vv], axis=0))
            for e in range(2):
                cst[:, pr * 2 + e] = 8.0 * slopes[h0 + e]
                sk[64 * e:64 * e + 64, pr] = sinks[h0 + e]
        maps.append(dict(q_in=q_in, k_in=k_in, v_in=v_in, dneg=dneg, cst=cst, sink=sk))
    res = _run(nc, maps)
    oT = [np.empty((D, SEQ), NPBF16) for _ in range(BATCH)]
    for c in range(NCORES):
        b_, hq = c // 4, c % 4
        oT[b_][hq * 256:(hq + 1) * 256] = np.asarray(res[c]["oT"]).reshape(256, SEQ)
    return oT


def _attn_B(pT, lam, subln, layer_idx):
    li = 0.8 - 0.6 * math.exp(-0.3 * layer_idx)
    nc = _prog(("diff", layer_idx), lambda: build_diff(li))
    slopes = alibi_slopes(8)
    mtab = diff_mtab()
    lamb = np.ascontiguousarray(np.broadcast_to(np.asarray(lam, dtype=np.float32)[None], (128, 4, 64)))
    sub = _f32(subln).reshape(128, 1)
    maps = []
    for c in range(NCORES):
        b_, hq = c // 4, c % 4
        P = pT[b_]
        q_in = np.empty((2, 128, SEQ), NPBF16)
        k_in = np.empty((2, 128, SEQ), NPBF16)
        v_in = np.empty((2, 128, SEQ // 128, 128), NPBF16)
        cst = np.zeros((128, 130), np.float32)
        for i in range(2):
            h = 2 * hq + i
            q_in[i] = P[h * 128:(h + 1) * 128]
            k_in[i] = P[1024 + h * 128:1024 + (h + 1) * 128]
            v_in[i] = v_layout(P[2048 + h * 128:2048 + (h + 1) * 128])
            cst[:, i] = 8.0 * slopes[h]
            cst[:, 2 + i * 64:2 + (i + 1) * 64] = (-slopes[h] * 128.0 * np.arange(64))[None, :]
        maps.append(dict(q_in=q_in, k_in=k_in, v_in=v_in, mtab=mtab, cst=cst, lam=lamb, sub=sub))
    res = _run(nc, maps)
    oT = [np.empty((D, SEQ), NPBF16) for _ in range(BATCH)]
    for c in range(NCORES):
        b_, hq = c // 4, c % 4
        oT[b_][hq * 256:(hq + 1) * 256] = np.asarray(res[c]["oT"]).reshape(256, SEQ)
    return oT


def kernel(x, norm_pre, norm_post, ffn_w_gate, ffn_w_up, ffn_w_down, a_w_in, a_w_out, b_w_in, b_w_out,
           b_lambda, b_subln, c_w_in, c_b_in, c_w_out, c_sinks):
    x = np.asarray(x, dtype=np.float32)
    norm_pre = np.asarray(norm_pre, dtype=np.float32)
    norm_post = np.asarray(norm_post, dtype=np.float32)
    depth = norm_pre.shape[0]
    xTs = []
    for c in range(NCORES):
        b_, ch = c // 4, c % 4
        xTs.append(np.ascontiguousarray(x[b_, ch * TPC:(ch + 1) * TPC, :].T))
    for i in range(depth):
        xTs = _ffn(xTs, ffn_w_gate[i, 0], ffn_w_up[i, 0], ffn_w_down[i, 0], norm_pre[i, 0], norm_post[i, 0])
        kind, j = i % 3, i // 3
        if kind == 0:
            pT = _proj(xTs, np.asarray(a_w_in[j]), norm_pre[i, 1])
            oT = _attn_A(pT)
            w_out = a_w_out[j]
        elif kind == 1:
            pT = _proj(xTs, np.asarray(b_w_in[j]), norm_pre[i, 1])
            oT = _attn_B(pT, b_lambda[j], b_subln[j], i)
            w_out = b_w_out[j]
        else:
            pT = _proj(xTs, np.asarray(c_w_in[j]), norm_pre[i, 1], b=np.asarray(c_b_in[j]))
            oT = _attn_C(pT, c_sinks[j])
            w_out = c_w_out[j]
        del pT
        xTs = _outproj(xTs, oT, np.asarray(w_out), norm_post[i, 1])
        xTs = _ffn(xTs, ffn_w_gate[i, 1], ffn_w_up[i, 1], ffn_w_down[i, 1], norm_pre[i, 2], norm_post[i, 2])
    out = np.empty((BATCH, SEQ, D), np.float32)
    for c in range(NCORES):
        b_, ch = c // 4, c % 4
        out[b_, ch * TPC:(ch + 1) * TPC, :] = xTs[c].T
    return out
```

```python
import math
import numpy as np
import ml_dtypes
import concourse.bass as bass
import concourse.mybir as mybir
from concourse.bass_utils import run_bass_kernel_spmd

F32 = mybir.dt.float32
BF16 = mybir.dt.bfloat16
AF = mybir.ActivationFunctionType
ALU = mybir.AluOpType
NPBF16 = ml_dtypes.bfloat16

D = 1024
DFF = 2816
NCH = D // 128
NFC = DFF // 128
EPS = 1e-6
NCORES = 8
SEQ = 8192
BATCH = 2
TPC = 2048


class Res:
    __slots__ = ("w", "r", "name")

    def __init__(self, name=""):
        self.w = None
        self.r = []
        self.name = name


class Sched:
    ENGS = ("pe", "act", "dve", "pool", "sp")
    NDMASEM = 24

    def __init__(self, nc):
        self.nc = nc
        self.ops = {e: [] for e in self.ENGS}
        self.dma_count = [0] * self.NDMASEM
        self.dma_last = [None] * self.NDMASEM
        self.ndma = 0
        self.epoch = 0

    def new_epoch(self):
        self.epoch += 1

    def res(self, name=""):
        return Res(name)

    def barrier(self):
        lasts = []
        for e in self.ENGS:
            for op in reversed(self.ops[e]):
                if op["fn"] is not None:
                    lasts.append(op)
                    break
        lasts += [d for d in self.dma_last if d is not None]
        if getattr(self, "cc_last", None) is not None:
            lasts.append(self.cc_last)
        for e in self.ENGS:
            self.add(e, None, extra_deps=lasts)

    def add_cc(self, fn, reads=(), writes=()):
        op = self.add("pool", fn, reads=reads, writes=writes)
        self.cc_count = getattr(self, "cc_count", 0) + 1
        op["cc"] = self.cc_count
        self.cc_last = op
        return op

    def add(self, eng, fn, reads=(), writes=(), dma=False, extra_deps=()):
        deps = list(extra_deps)
        for r in reads:
            if r.w is not None:
                deps.append(r.w)
        for r in writes:
            if r.w is not None:
                deps.append(r.w)
            deps.extend(r.r)
        op = {"eng": eng, "fn": fn, "deps": deps, "signal": False, "dma": None, "cc": None, "idx": len(self.ops[eng]), "epoch": self.epoch}
        if dma:
            s = self.ndma % self.NDMASEM
            self.ndma += 1
            if self.dma_last[s] is not None:
                op["deps"].append(self.dma_last[s])
            self.dma_count[s] += 1
            op["dma"] = (s, 16 * self.dma_count[s])
            self.dma_last[s] = op
        self.ops[eng].append(op)
        for r in reads:
            r.r.append(op)
        for r in writes:
            r.w = op
            r.r = []
        return op

    def emit(self):
        nc = self.nc
        for e in self.ENGS:
            for op in self.ops[e]:
                ded = []
                seen = set()
                best = {}
                for d in op["deps"]:
                    if id(d) in seen:
                        continue
                    seen.add(id(d))
                    if d["dma"] is None and d["cc"] is None:
                        if d["eng"] == "pe" and e == "pe":
                            continue
                        if d["epoch"] < op["epoch"]:
                            continue
                        if d["eng"] not in best or d["idx"] > best[d["eng"]]["idx"]:
                            best[d["eng"]] = d
                    else:
                        ded.append(d)
                for d in best.values():
                    d["signal"] = True
                    ded.append(d)
                op["deps"] = ded
        for e in self.ENGS:
            c = 0
            ep = 0
            for op in self.ops[e]:
                if op["epoch"] != ep:
                    ep = op["epoch"]
                    c = 0
                if op["signal"]:
                    c += 1
                op["cnt"] = c
        import contextlib
        with contextlib.ExitStack() as st:
            esem = {(ep, e): st.enter_context(nc.semaphore("s_%s_%d" % (e, ep))) for e in self.ENGS for ep in range(self.epoch + 1)}
            dsem = [st.enter_context(nc.semaphore("d_%d" % i)) for i in range(self.NDMASEM)]
            ccsem = st.enter_context(nc.semaphore("s_cc"))
            block = st.enter_context(nc.Block())

            def run(e, eng):
                waited = {}
                for op in self.ops[e]:
                    need = {}
                    for d in op["deps"]:
                        if d["dma"] is not None:
                            key, val = ("d", d["dma"][0]), d["dma"][1]
                        elif d["cc"] is not None:
                            key, val = ("c", 0), d["cc"]
                        else:
                            key, val = ("e", (d["epoch"], d["eng"])), d["cnt"]
                        if val > waited.get(key, 0) and val > need.get(key, 0):
                            need[key] = val
                    for key, val in need.items():
                        sem = dsem[key[1]] if key[0] == "d" else (ccsem if key[0] == "c" else esem[key[1]])
                        eng.wait_ge(sem, val)
                        waited[key] = val
                    if op["fn"] is None:
                        continue
                    ins = op["fn"](eng)
                    if op["cc"] is not None:
                        ins.then_inc(ccsem, 1)
                    elif op["dma"] is not None:
                        ins.then_inc(dsem[op["dma"][0]], 16)
                    elif op["signal"]:
                        ins.then_inc(esem[(op["epoch"], e)], 1)
                if e == "sp":
                    for s in range(self.NDMASEM):
                        if self.dma_count[s] > 0:
                            eng.wait_ge(dsem[s], 16 * self.dma_count[s])
                    if getattr(self, "cc_count", 0) > 0:
                        eng.wait_ge(ccsem, self.cc_count)

            @block.tensor
            def _(eng):
                run("pe", eng)

            @block.scalar
            def _(eng):
                run("act", eng)

            @block.vector
            def _(eng):
                run("dve", eng)

            @block.gpsimd
            def _(eng):
                if getattr(self, "want_pid", False):
                    self.pid = eng.partition_id()
                run("pool", eng)

            @block.sync
            def _(eng):
                run("sp", eng)


class Ctx:
    def __init__(self):
        self.nc = bass.Bass("TRN2", target_bir_lowering=False)
        self.s = Sched(self.nc)
        self.n = 0
        self.uid = 0
        nc = self.nc
        self.ps = [nc.alloc_psum_tensor("ps%d" % i, [128, 512], F32) for i in range(8)]
        self.psr = [Res("ps%d" % i) for i in range(8)]
        self.ones_f = nc.alloc_sbuf_tensor("ones_f", [128, 128], F32)
        self.ones_r = Res("ones_f")
        self.s.add("pool", lambda e: e.memset(self.ones_f[:, :], 1.0), writes=[self.ones_r])
        self.eps_col = nc.alloc_sbuf_tensor("eps", [128, 1], F32)
        self.eps_r = Res()
        self.s.add("pool", lambda e: e.memset(self.eps_col[:, :], EPS), writes=[self.eps_r])
        self.abase = None

    def perm(self, name, shape, dt):
        assert self.abase is None
        return self.nc.alloc_sbuf_tensor(name, shape, dt)

    def arena_start(self):
        self.abase = (self.nc.sbuf_base + 63) // 64 * 64
        self.atop = self.nc.sbuf_top
        self.apos = self.abase

    def phase(self):
        self.s.barrier()
        self.apos = self.abase
        self.psr = [Res("ps%d" % i) for i in range(8)]

    def ar(self, name, shape, dt):
        sz = 4 if dt == F32 else 2
        nb = sz
        for d_ in shape[1:]:
            nb *= d_
        nb = (nb + 63) // 64 * 64
        self.uid += 1
        t = self.nc.alloc_sbuf_tensor_at("%s_%d" % (name, self.uid), list(shape), dt, offset=self.apos)
        self.apos += nb
        assert self.apos <= self.atop, ("arena overflow", name, self.apos, self.atop)
        return t

    def dram_in(self, name, shape, dt=F32):
        return self.nc.dram_tensor(name, list(shape), dt, kind="ExternalInput")

    def dram_out(self, name, shape, dt=F32):
        return self.nc.dram_tensor(name, list(shape), dt, kind="ExternalOutput")

    def dram(self, name, shape, dt):
        return self.nc.dram_tensor(name, list(shape), dt)


def emit_rstd(cx, src_fn, nsrc, ncols, rstd, rstd_r, src_reads, sq, sq_r, psi, inv_n=1.0 / D):
    s = cx.s
    ps, psr = cx.ps[psi], cx.psr[psi]
    for c in range(nsrc):
        s.add("act", lambda e, c=c: e.activation(out=sq[:, c % 2, :ncols], in_=src_fn(c), func=AF.Square),
              reads=src_reads(c), writes=[sq_r[c % 2]])
        s.add("pe", lambda e, c=c: e.matmul(ps[:, :ncols], cx.ones_f[:, :], sq[:, c % 2, :ncols],
                                            start=(c == 0), stop=(c == nsrc - 1)),
              reads=[sq_r[c % 2], cx.ones_r], writes=[psr])
    s.add("act", lambda e: e.activation(out=rstd[:, :ncols], in_=ps[:, :ncols], func=AF.Sqrt,
                                        scale=inv_n, bias=cx.eps_col[:, 0:1]),
          reads=[psr, cx.eps_r], writes=[rstd_r])
    s.add("dve", lambda e: e.reciprocal(out=rstd[:, :ncols], in_=rstd[:, :ncols]),
          reads=[rstd_r], writes=[rstd_r])


def emit_ffn(cx, x, x_r, wg_ap, wu_ap, wd_ap, g1col, g2col, g_r):
    s = cx.s
    T, HALF = TPC, 1024
    cx.phase()
    xn = cx.ar("xn", [128, NCH, HALF], BF16)
    hT = cx.ar("hT", [128, NFC, HALF], BF16)
    fT = cx.ar("fT", [128, NCH, HALF], F32)
    sq = cx.ar("sq", [128, 2, 512], F32)
    rstd = cx.ar("rstd", [128, 512], F32)
    sg = cx.ar("sg", [128, 2, 512], F32)
    tmp = cx.ar("tmp", [128, 2, 512], F32)
    NWS, NDS = 2, 3
    wgs = cx.ar("wgs", [128, NWS, NCH, 256], BF16)
    wus = cx.ar("wus", [128, NWS, NCH, 256], BF16)
    wds = cx.ar("wds", [128, NDS, NFC, 128], BF16)
    xn_r = [[Res() for _ in range(2)] for _ in range(NCH)]
    hT_r = [[Res() for _ in range(2)] for _ in range(NFC)]
    fT_r = [[Res() for _ in range(2)] for _ in range(NCH)]
    sq_r, sg_r, tmp_r = [Res(), Res()], [Res(), Res()], [Res(), Res()]
    rstd_r = Res()
    wgs_r = [Res() for _ in range(NWS)]
    wus_r = [Res() for _ in range(NWS)]
    wds_r = [Res() for _ in range(NDS)]
    wg_v = wg_ap.rearrange("(c p) f -> p c f", p=128)
    wu_v = wu_ap.rearrange("(c p) f -> p c f", p=128)
    wd_v = wd_ap.rearrange("(j p) d -> p j d", p=128)
    wcount = [0, 0]
    for h in range(T // HALF):
        for tt in range(2):
            t = h * 2 + tt
            cs = slice(t * 512, (t + 1) * 512)
            ls = slice(tt * 512, (tt + 1) * 512)
            emit_rstd(cx, lambda c, cs=cs: x[:, c, cs], NCH, 512, rstd, rstd_r,
                      lambda c, t=t: [x_r[c][t]], sq, sq_r, 6)
            for c in range(NCH):
                s.add("dve", lambda e, c=c, cs=cs, ls=ls: e.scalar_tensor_tensor(
                    out=xn[:, c, ls], in0=x[:, c, cs], scalar=g1col(c), in1=rstd[:, :],
                    op0=ALU.mult, op1=ALU.mult),
                    reads=[x_r[c][t], g_r, rstd_r], writes=[xn_r[c][tt]])
        for jj in range(NFC // 2):
            slot = wcount[0] % NWS
            wcount[0] += 1
            s.add("pool", lambda e, jj=jj, slot=slot: e.dma_start(out=wgs[:, slot, :, :], in_=wg_v[:, :, jj * 256:(jj + 1) * 256]),
                  writes=[wgs_r[slot]], dma=True)
            s.add("pool", lambda e, jj=jj, slot=slot: e.dma_start(out=wus[:, slot, :, :], in_=wu_v[:, :, jj * 256:(jj + 1) * 256]),
                  writes=[wus_r[slot]], dma=True)
            for tt in range(2):
                ls = slice(tt * 512, (tt + 1) * 512)
                for j2 in range(2):
                    j = jj * 2 + j2
                    k = cx.n % 2
                    cx.n += 1
                    pg, pu = 2 * k, 2 * k + 1
                    for c in range(NCH):
                        s.add("pe", lambda e, c=c, slot=slot, j2=j2, ls=ls, pg=pg: e.matmul(
                            cx.ps[pg][:, :], wgs[:, slot, c, j2 * 128:(j2 + 1) * 128], xn[:, c, ls],
                            start=(c == 0), stop=(c == NCH - 1)),
                            reads=[wgs_r[slot], xn_r[c][tt]], writes=[cx.psr[pg]])
                    for c in range(NCH):
                        s.add("pe", lambda e, c=c, slot=slot, j2=j2, ls=ls, pu=pu: e.matmul(
                            cx.ps[pu][:, :], wus[:, slot, c, j2 * 128:(j2 + 1) * 128], xn[:, c, ls],
                            start=(c == 0), stop=(c == NCH - 1)),
                            reads=[wus_r[slot], xn_r[c][tt]], writes=[cx.psr[pu]])
                    s.add("act", lambda e, k=k, pg=pg: e.activation(out=sg[:, k, :], in_=cx.ps[pg][:, :], func=AF.Silu),
                          reads=[cx.psr[pg]], writes=[sg_r[k]])
                    s.add("dve", lambda e, k=k, pu=pu, j=j, ls=ls: e.tensor_tensor(
                        out=hT[:, j, ls], in0=sg[:, k, :], in1=cx.ps[pu][:, :], op=ALU.mult),
                        reads=[sg_r[k], cx.psr[pu]], writes=[hT_r[j][tt]])
        for c in range(NCH):
            slot = wcount[1] % NDS
            wcount[1] += 1
            s.add("pool", lambda e, c=c, slot=slot: e.dma_start(out=wds[:, slot, :, :], in_=wd_v[:, :, c * 128:(c + 1) * 128]),
                  writes=[wds_r[slot]], dma=True)
            for tt in range(2):
                ls = slice(tt * 512, (tt + 1) * 512)
                pd = 4 + (cx.n % 2)
                cx.n += 1
                for j in range(NFC):
                    s.add("pe", lambda e, j=j, slot=slot, ls=ls, pd=pd: e.matmul(
                        cx.ps[pd][:, :], wds[:, slot, j, :], hT[:, j, ls],
                        start=(j == 0), stop=(j == NFC - 1)),
                        reads=[wds_r[slot], hT_r[j][tt]], writes=[cx.psr[pd]])
                s.add("act", lambda e, c=c, ls=ls, pd=pd: e.activation(out=fT[:, c, ls], in_=cx.ps[pd][:, :], func=AF.Copy),
                      reads=[cx.psr[pd]], writes=[fT_r[c][tt]])
        for tt in range(2):
            t = h * 2 + tt
            cs = slice(t * 512, (t + 1) * 512)
            ls = slice(tt * 512, (tt + 1) * 512)
            emit_rstd(cx, lambda c, ls=ls: fT[:, c, ls], NCH, 512, rstd, rstd_r,
                      lambda c, tt=tt: [fT_r[c][tt]], sq, sq_r, 7)
            for c in range(NCH):
                k = c % 2
                s.add("dve", lambda e, c=c, ls=ls, k=k: e.scalar_tensor_tensor(
                    out=tmp[:, k, :], in0=fT[:, c, ls], scalar=g2col(c), in1=rstd[:, :],
                    op0=ALU.mult, op1=ALU.mult),
                    reads=[fT_r[c][tt], g_r, rstd_r], writes=[tmp_r[k]])
                s.add("pool", lambda e, c=c, cs=cs, k=k: e.tensor_tensor(
                    out=x[:, c, cs], in0=x[:, c, cs], in1=tmp[:, k, :], op=ALU.add),
                    reads=[tmp_r[k]], writes=[x_r[c][t]])


def emit_hn_gather(cx, x, x_r, gcol, g_r, hn_snd, hn_all):
    s = cx.s
    cx.phase()
    sq = cx.ar("sq", [128, 2, 512], F32)
    rstd = cx.ar("rstd", [128, 512], F32)
    hst = cx.ar("hst", [128, 2, NCH, 512], BF16)
    sq_r, rstd_r = [Res(), Res()], Res()
    hst_r = [[Res() for _ in range(NCH)] for _ in range(2)]
    outs = []
    for t in range(TPC // 512):
        k = t % 2
        cs = slice(t * 512, (t + 1) * 512)
        emit_rstd(cx, lambda c, cs=cs: x[:, c, cs], NCH, 512, rstd, rstd_r,
                  lambda c, t=t: [x_r[c][t]], sq, sq_r, 6 + k)
        for c in range(NCH):
            s.add("dve", lambda e, c=c, cs=cs, k=k: e.scalar_tensor_tensor(
                out=hst[:, k, c, :], in0=x[:, c, cs], scalar=gcol(c), in1=rstd[:, :],
                op0=ALU.mult, op1=ALU.mult),
                reads=[x_r[c][t], g_r, rstd_r], writes=[hst_r[k][c]])
            r_ = Res()
            s.add("sp", lambda e, c=c, cs=cs, k=k: e.dma_start(out=hn_snd.ap()[c * 128:(c + 1) * 128, cs], in_=hst[:, k, c, :]),
                  reads=[hst_r[k][c]], writes=[r_], dma=True)
            outs.append(r_)
    all_r = Res()
    s.add_cc(lambda e: e.collective_compute("AllGather", ALU.bypass, replica_groups=[[0, 1, 2, 3], [4, 5, 6, 7]],
                                            ins=[hn_snd.ap()], outs=[hn_all.ap()]),
             reads=outs, writes=[all_r])
    return all_r


def emit_proj_heads(cx, hn_all, hn_all_r, w_ap, nblk, dil_of_blk, scratch, bias_ap=None):
    s = cx.s
    cx.phase()
    xn = cx.ar("xnc", [128, NCH, TPC], BF16)
    NWS, NOS = 3, 3
    ws = cx.ar("pws", [128, NWS, NCH, 256], BF16)
    ost = cx.ar("post", [128, NOS, TPC], BF16)
    xn_r = [Res() for _ in range(NCH)]
    ws_r = [Res() for _ in range(NWS)]
    ost_r = [Res() for _ in range(NOS)]
    if bias_ap is not None:
        bcol = cx.ar("bcol", [128, nblk], F32)
        b_r = Res()
        s.add("sp", lambda e: e.dma_start(out=bcol[:, :], in_=bias_ap), writes=[b_r], dma=True)
    w_v = w_ap.rearrange("(c p) f -> p c f", p=128)
    blk_r = [[] for _ in range(nblk)]
    n = 0
    wn = 0
    on = 0
    for ch in range(4):
        for c in range(NCH):
            s.add("sp", lambda e, c=c, ch=ch: e.dma_start(out=xn[:, c, :], in_=hn_all.ap()[c * 512 + ch * 128:c * 512 + (ch + 1) * 128, :]),
                  reads=[hn_all_r[c]], writes=[xn_r[c]], dma=True)
        for mm in range(nblk // 2):
            slot = wn % NWS
            wn += 1
            s.add("pool", lambda e, mm=mm, slot=slot: e.dma_start(out=ws[:, slot, :, :], in_=w_v[:, :, mm * 256:(mm + 1) * 256]),
                  writes=[ws_r[slot]], dma=True)
            for m2 in range(2):
                m = mm * 2 + m2
                d = dil_of_blk[m]
                os_ = on % NOS
                on += 1
                for t in range(4):
                    cs = slice(t * 512, (t + 1) * 512)
                    pi = n % 6
                    n += 1
                    for c in range(NCH):
                        s.add("pe", lambda e, c=c, slot=slot, m2=m2, cs=cs, pi=pi: e.matmul(
                            cx.ps[pi][:, :], ws[:, slot, c, m2 * 128:(m2 + 1) * 128], xn[:, c, cs],
                            start=(c == 0), stop=(c == NCH - 1)),
                            reads=[ws_r[slot], xn_r[c]], writes=[cx.psr[pi]])
                    if d == 1:
                        o_ap = ost[:, os_, cs]
                        i_ap = cx.ps[pi][:, :]
                    else:
                        w_ = 512 // d
                        o_ap = ost[:, os_, :].rearrange("p (r m) -> p r m", r=d)[:, :, t * w_:(t + 1) * w_].rearrange("p r m -> p m r")
                        i_ap = cx.ps[pi][:, :].rearrange("p (m r) -> p m r", r=d)
                    if bias_ap is not None:
                        s.add("act", lambda e, o_ap=o_ap, i_ap=i_ap, m=m: e.activation(
                            out=o_ap, in_=i_ap, func=AF.Identity, bias=bcol[:, m:m + 1]),
                            reads=[cx.psr[pi], b_r], writes=[ost_r[os_]])
                    elif n % 2 == 0:
                        s.add("act", lambda e, o_ap=o_ap, i_ap=i_ap: e.activation(out=o_ap, in_=i_ap, func=AF.Copy),
                              reads=[cx.psr[pi]], writes=[ost_r[os_]])
                    else:
                        s.add("dve", lambda e, o_ap=o_ap, i_ap=i_ap: e.tensor_copy(out=o_ap, in_=i_ap),
                              reads=[cx.psr[pi]], writes=[ost_r[os_]])
                r_ = Res()
                if d == 1:
                    dst = scratch.ap()[m * 128:(m + 1) * 128, ch * TPC:(ch + 1) * TPC]
                    src = ost[:, os_, :]
                else:
                    dst = scratch.ap()[m * 128:(m + 1) * 128, :].rearrange("p (r c m) -> p r c m", r=d, c=4)[:, :, ch, :]
                    src = ost[:, os_, :].rearrange("p (r m) -> p r m", r=d)
                s.add("sp", lambda e, dst=dst, src=src: e.dma_start(out=dst, in_=src),
                      reads=[ost_r[os_]], writes=[r_], dma=True)
                blk_r[m].append(r_)
    return blk_r


def emit_o_scatter(cx, o_rs_in, o_parts_r, o_my):
    r_ = Res()
    cx.s.add_cc(lambda e: e.collective_compute("ReduceScatter", ALU.add, replica_groups=[[0, 1, 2, 3], [4, 5, 6, 7]],
                                               ins=[o_rs_in.ap()], outs=[o_my.ap()]),
                reads=o_parts_r, writes=[r_])
    return r_


def emit_banded(cx, scratch, blk_r, qblk, kblk, vblk, dils, dneg_ap, cst_ap, sink_ap, ident, ident_r, o_dst):
    s = cx.s
    cx.phase()
    G = len(dils)
    NB = SEQ // 128
    dneg = cx.ar("dneg", [128, 256], F32)
    dneg_r = Res()
    s.add("sp", lambda e: e.dma_start(out=dneg[:, :], in_=dneg_ap), writes=[dneg_r], dma=True)
    cst = cx.ar("cstb", [128, 2 * G * 2], F32)
    cst_r = Res()
    s.add("sp", lambda e: e.dma_start(out=cst[:, :], in_=cst_ap), writes=[cst_r], dma=True)
    has_sink = sink_ap is not None
    if has_sink:
        sk = cx.ar("sk", [128, 2], F32)
        sk_r = Res()
        s.add("sp", lambda e: e.dma_start(out=sk[:, :], in_=sink_ap), writes=[sk_r], dma=True)
        s.add("act", lambda e: e.activation(out=sk[:, :], in_=sk[:, :], func=AF.Exp), reads=[sk_r], writes=[sk_r])
    ones_b = cx.ar("ones_b", [128, 64], BF16)
    ones_b_r = Res()
    s.add("pool", lambda e: e.memset(ones_b[:, :], 1.0), writes=[ones_b_r])
    NQS = 4
    qs = cx.ar("qs", [128, NQS, 2048], BF16)
    ks = cx.ar("ks", [128, NQS, 2048], BF16)
    vs = cx.ar("vs", [128, NQS, 16, 128], BF16)
    vf = cx.ar("vf", [128, 2, 2048], BF16)
    qs_r = [Res() for _ in range(NQS)]
    ks_r = [Res() for _ in range(NQS)]
    vs_r = [Res() for _ in range(NQS)]
    vf_r = [Res(), Res()]
    accn = cx.ar("accn", [128, SEQ], F32)
    accd = cx.ar("accd", [128, SEQ], F32)
    acc_r = [Res() for _ in range(NB)]
    NP = 4
    pb = cx.ar("pb", [128, NP, 2, 256], BF16)
    pb_r = [[Res(), Res()] for _ in range(NP)]
    NSB = 4
    sbb = cx.ar("sbb", [128, NSB, 256], F32)
    sbb_r = [Res() for _ in range(NSB)]
    ost = cx.ar("ost", [128, 2, 2048], BF16)
    ost_r = [Res(), Res()]
    out_rs = []
    it = 0
    nS = 0
    nPV = 0
    nV = 0
    for pr in range(2):
        for g in range(G):
            d = dils[g]
            nbc = NB // d
            slot_of = lambda Q, it=it: (it * 4 + Q) % NQS
            qb, kb_, vb = qblk[pr][g], kblk[pr][g], vblk[pr][g]
            for Q in range(4):
                sl = slot_of(Q)
                s.add("sp", lambda e, qb=qb, Q=Q, sl=sl: e.dma_start(
                    out=qs[:, sl, :], in_=scratch.ap()[qb * 128:(qb + 1) * 128, Q * 2048:(Q + 1) * 2048]),
                    reads=blk_r[qb], writes=[qs_r[sl]], dma=True)
                s.add("sp", lambda e, kb_=kb_, Q=Q, sl=sl: e.dma_start(
                    out=ks[:, sl, :], in_=scratch.ap()[kb_ * 128:(kb_ + 1) * 128, Q * 2048:(Q + 1) * 2048]),
                    reads=blk_r[kb_], writes=[ks_r[sl]], dma=True)
                vk = nV % 2
                nV += 1
                s.add("sp", lambda e, vb=vb, Q=Q, vk=vk: e.dma_start(
                    out=vf[:, vk, :], in_=scratch.ap()[vb * 128:(vb + 1) * 128, Q * 2048:(Q + 1) * 2048]),
                    reads=blk_r[vb], writes=[vf_r[vk]], dma=True)
                for q4 in range(4):
                    pi = 6 + (q4 % 2)
                    pst = cx.ps[pi][:, :].bitcast(BF16)
                    for j in range(4):
                        lb = q4 * 4 + j
                        s.add("pe", lambda e, pst=pst, j=j, lb=lb, vk=vk: e.transpose(
                            pst[:, j * 128:(j + 1) * 128], vf[:, vk, lb * 128:(lb + 1) * 128], ident[:, :]),
                            reads=[vf_r[vk], ident_r], writes=[cx.psr[pi]])
                    s.add("dve" if q4 % 2 == 0 else "act",
                          (lambda e, pst=pst, sl=sl, q4=q4: e.tensor_copy(out=vs[:, sl, q4 * 4:(q4 + 1) * 4, :], in_=pst[:, 0:512].rearrange("p (j f) -> p j f", j=4)))
                          if q4 % 2 == 0 else
                          (lambda e, pst=pst, sl=sl, q4=q4: e.activation(out=vs[:, sl, q4 * 4:(q4 + 1) * 4, :], in_=pst[:, 0:512].rearrange("p (j f) -> p j f", j=4), func=AF.Copy)),
                          reads=[cx.psr[pi]], writes=[vs_r[sl]])

            def emit_S(B, pr=pr, g=g, d=d, nbc=nbc, slot_of=slot_of):
                nonlocal nS
                b = B % nbc
                last = (b == nbc - 1)
                n = 128 if last else 256
                Q, lb = B // 16, B % 16
                sl = slot_of(Q)
                for e_ in range(2):
                    pi = nS % 4
                    si = nS % NSB
                    nS += 1
                    pslot = B % NP
                    rows = slice(64 * e_, 64 * e_ + 64)
                    s.add("pe", lambda e, pi=pi, sl=sl, lb=lb, rows=rows: e.matmul(
                        cx.ps[pi][:, 0:128], ks[rows, sl, lb * 128:(lb + 1) * 128], qs[rows, sl, lb * 128:(lb + 1) * 128],
                        start=True, stop=True), reads=[ks_r[sl], qs_r[sl]], writes=[cx.psr[pi]])
                    if not last:
                        Q2, lb2 = (B + 1) // 16, (B + 1) % 16
                        sl2 = slot_of(Q2)
                        s.add("pe", lambda e, pi=pi, sl=sl, sl2=sl2, lb=lb, lb2=lb2, rows=rows: e.matmul(
                            cx.ps[pi][:, 128:256], ks[rows, sl, lb * 128:(lb + 1) * 128], qs[rows, sl2, lb2 * 128:(lb2 + 1) * 128],
                            start=True, stop=True), reads=[ks_r[sl], qs_r[sl2]], writes=[cx.psr[pi]])
                    ci = (pr * G + g) * 2 + e_
                    s.add("dve", lambda e, si=si, pi=pi, n=n, ci=ci: e.scalar_tensor_tensor(
                        out=sbb[:, si, 0:n], in0=dneg[:, 0:n], scalar=cst[:, ci:ci + 1], in1=cx.ps[pi][:, 0:n],
                        op0=ALU.mult, op1=ALU.add), reads=[dneg_r, cst_r, cx.psr[pi]], writes=[sbb_r[si]])
                    s.add("act", lambda e, si=si, pslot=pslot, e_=e_, n=n: e.activation(
                        out=pb[:, pslot, e_, 0:n], in_=sbb[:, si, 0:n], func=AF.Exp, scale=0.125),
                        reads=[sbb_r[si]], writes=[pb_r[pslot][e_]])

            def emit_PV(B, pr=pr, g=g, d=d, nbc=nbc, slot_of=slot_of):
                nonlocal nPV
                b = B % nbc
                r = B // nbc
                kcol = 128 * (nPV % 2)
                nPV += 1
                pn, pd_ = 4, 5
                ncol = slice(kcol, kcol + 128)
                srcs = []
                if b > 0:
                    srcs.append((B - 1, slice(128, 256)))
                srcs.append((B, slice(0, 128)))
                for e_ in range(2):
                    orow = slice(64 * e_, 64 * e_ + 64)
                    for i, (kb, cols) in enumerate(srcs):
                        Qk, lbk = kb // 16, kb % 16
                        slk = slot_of(Qk)
                        s.add("pe", lambda e, orow=orow, slk=slk, lbk=lbk, kb=kb, e_=e_, cols=cols, i=i, ncol=ncol: e.matmul(
                            cx.ps[pn][orow, ncol], vs[:, slk, lbk, 64 * e_:64 * e_ + 64], pb[:, kb % NP, e_, cols],
                            start=(i == 0), stop=(i == len(srcs) - 1)),
                            reads=[vs_r[slk], pb_r[kb % NP][e_]], writes=[cx.psr[pn]])
                for e_ in range(2):
                    orow = slice(64 * e_, 64 * e_ + 64)
                    for i, (kb, cols) in enumerate(srcs):
                        s.add("pe", lambda e, orow=orow, kb=kb, e_=e_, cols=cols, i=i, ncol=ncol: e.matmul(
                            cx.ps[pd_][orow, ncol], ones_b[:, :], pb[:, kb % NP, e_, cols],
                            start=(i == 0), stop=(i == len(srcs) - 1)),
                            reads=[ones_b_r, pb_r[kb % NP][e_]], writes=[cx.psr[pd_]])
                off = b * 128 * d + r
                view = slice(off, off + 127 * d + 1, d) if d > 1 else slice(off, off + 128)
                ar = [acc_r[j] for j in range(b * d, (b + 1) * d)]
                if g == 0:
                    s.add("act", lambda e, view=view, ncol=ncol: e.activation(out=accn[:, view], in_=cx.ps[pn][:, ncol], func=AF.Copy),
                          reads=[cx.psr[pn]], writes=ar)
                    if has_sink:
                        s.add("dve", lambda e, view=view, pr=pr, ncol=ncol: e.tensor_scalar(
                            out=accd[:, view], in0=cx.ps[pd_][:, ncol], scalar1=sk[:, pr:pr + 1], scalar2=None, op0=ALU.add),
                            reads=[cx.psr[pd_], sk_r], writes=ar)
                    else:
                        s.add("dve", lambda e, view=view, ncol=ncol: e.tensor_copy(out=accd[:, view], in_=cx.ps[pd_][:, ncol]),
                              reads=[cx.psr[pd_]], writes=ar)
                else:
                    s.add("dve", lambda e, view=view, ncol=ncol: e.tensor_tensor(
                        out=accn[:, view], in0=accn[:, view], in1=cx.ps[pn][:, ncol], op=ALU.add),
                        reads=[cx.psr[pn]], writes=ar)
                    s.add("dve", lambda e, view=view, ncol=ncol: e.tensor_tensor(
                        out=accd[:, view], in0=accd[:, view], in1=cx.ps[pd_][:, ncol], op=ALU.add),
                        reads=[cx.psr[pd_]], writes=ar)

            emit_S(0)
            for B in range(NB):
                if B + 1 < NB:
                    emit_S(B + 1)
                emit_PV(B)
            it += 1
        for Q in range(4):
            cs = slice(Q * 2048, (Q + 1) * 2048)
            ar = [acc_r[j] for j in range(Q * 16, (Q + 1) * 16)]
            k = Q % 2
            s.add("dve", lambda e, cs=cs: e.reciprocal(out=accd[:, cs], in_=accd[:, cs]), reads=ar, writes=ar)
            s.add("pool", lambda e, cs=cs, k=k: e.tensor_tensor(out=ost[:, k, :], in0=accn[:, cs], in1=accd[:, cs], op=ALU.mult),
                  reads=ar, writes=[ost_r[k]])
            r_ = Res()
            s.add("pool", lambda e, pr=pr, Q=Q, k=k: e.dma_start(out=o_dst(pr, Q), in_=ost[:, k, :]),
                  reads=[ost_r[k]], writes=[r_], dma=True)
            out_rs.append(r_)
    return out_rs


def emit_diff(cx, scratch, blk_r, lambda_init, mt_ap, cst_ap, lam_ap, sub_ap, ident, ident_r, o_dst):
    s = cx.s
    cx.phase()
    NB = SEQ // 128
    NQT = SEQ // 512
    mtab = cx.ar("mtab", [128, 5, 512], F32)
    mtab_r = Res()
    s.add("sp", lambda e: e.dma_start(out=mtab[:, :, :], in_=mt_ap), writes=[mtab_r], dma=True)
    cst = cx.ar("cstd", [128, 2 + 128], F32)
    cst_r = Res()
    s.add("sp", lambda e: e.dma_start(out=cst[:, :], in_=cst_ap), writes=[cst_r], dma=True)
    lam = cx.ar("lam", [128, 4, 64], F32)
    lam_r = Res()
    s.add("sp", lambda e: e.dma_start(out=lam[:, :, :], in_=lam_ap), writes=[lam_r], dma=True)
    sub = cx.ar("sub", [128, 1], F32)
    sub_r = Res()
    s.add("sp", lambda e: e.dma_start(out=sub[:, :], in_=sub_ap), writes=[sub_r], dma=True)
    ones_b = cx.ar("ones_b", [128, 128], BF16)
    ones_b_r = Res()
    s.add("pool", lambda e: e.memset(ones_b[:, :], 1.0), writes=[ones_b_r])
    lp = cx.ar("lp", [128, 2, 64], F32)
    lsum = cx.ar("lsum", [128, 2], F32)
    neglam = cx.ar("neglam", [128, 1], F32)
    lp_r, lsum_r, neglam_r = Res(), Res(), Res()
    for i in range(2):
        s.add("dve", lambda e, i=i: e.tensor_tensor(out=lp[:, i, :], in0=lam[:, 2 * i, :], in1=lam[:, 2 * i + 1, :], op=ALU.mult),
              reads=[lam_r], writes=[lp_r])
        s.add("dve", lambda e, i=i: e.reduce_sum(out=lsum[:, i:i + 1], in_=lp[:, i, :], axis=mybir.AxisListType.X),
              reads=[lp_r], writes=[lsum_r])
    s.add("act", lambda e: e.activation(out=lsum[:, :], in_=lsum[:, :], func=AF.Exp), reads=[lsum_r], writes=[lsum_r])
    s.add("dve", lambda e: e.tensor_tensor(out=neglam[:, :], in0=lsum[:, 1:2], in1=lsum[:, 0:1], op=ALU.subtract),
          reads=[lsum_r], writes=[neglam_r])
    s.add("dve", lambda e: e.tensor_scalar(out=neglam[:, :], in0=neglam[:, :], scalar1=-float(lambda_init), scalar2=None, op0=ALU.add),
          reads=[neglam_r], writes=[neglam_r])
    qs = cx.ar("qs", [128, SEQ], BF16)
    ks = cx.ar("ks", [128, SEQ], BF16)
    vs = cx.ar("vs", [128, NB, 128], BF16)
    vf = cx.ar("vf", [128, 2, 2048], BF16)
    qs_r = [Res() for _ in range(4)]
    ks_r = [Res() for _ in range(4)]
    vs_r = [Res() for _ in range(4)]
    vf_r = [Res(), Res()]
    NP = 6
    pb = cx.ar("pb", [128, NP, 512], BF16)
    pb_r = [Res() for _ in range(NP)]
    NSB = 4
    sbb = cx.ar("sbb", [128, NSB, 512], F32)
    sbb_r = [Res() for _ in range(NSB)]
    rd = cx.ar("rd", [128, 2, 512], F32)
    rd_r = [Res(), Res()]
    tt_ = cx.ar("tt", [128, 2, 512], F32)
    tt_r = [Res(), Res()]
    ob = cx.ar("ob", [128, 512], F32)
    ob_r = Res()
    sq = cx.ar("sq", [128, 2, 512], F32)
    sq_r = [Res(), Res()]
    rstd = cx.ar("rstd", [128, 512], F32)
    rstd_r = Res()
    ost = cx.ar("ost", [128, 2, 512], BF16)
    ost_r = [Res(), Res()]
    out_rs = []
    nS = 0
    nV = 0
    for i in range(2):
        qb, kb_, vb = 3 * i, 3 * i + 1, 3 * i + 2
        for Q in range(4):
            s.add("sp", lambda e, qb=qb, Q=Q: e.dma_start(out=qs[:, Q * 2048:(Q + 1) * 2048], in_=scratch.ap()[qb * 128:(qb + 1) * 128, Q * 2048:(Q + 1) * 2048]),
                  reads=blk_r[qb], writes=[qs_r[Q]], dma=True)
            s.add("sp", lambda e, kb_=kb_, Q=Q: e.dma_start(out=ks[:, Q * 2048:(Q + 1) * 2048], in_=scratch.ap()[kb_ * 128:(kb_ + 1) * 128, Q * 2048:(Q + 1) * 2048]),
                  reads=blk_r[kb_], writes=[ks_r[Q]], dma=True)
            vk = nV % 2
            nV += 1
            s.add("sp", lambda e, vb=vb, Q=Q, vk=vk: e.dma_start(out=vf[:, vk, :], in_=scratch.ap()[vb * 128:(vb + 1) * 128, Q * 2048:(Q + 1) * 2048]),
                  reads=blk_r[vb], writes=[vf_r[vk]], dma=True)
            for q4 in range(4):
                pi = q4 % 4
                pst = cx.ps[pi][:, :].bitcast(BF16)
                for j in range(4):
                    lb = q4 * 4 + j
                    s.add("pe", lambda e, pst=pst, j=j, lb=lb, vk=vk: e.transpose(
                        pst[:, j * 128:(j + 1) * 128], vf[:, vk, lb * 128:(lb + 1) * 128], ident[:, :]),
                        reads=[vf_r[vk], ident_r], writes=[cx.psr[pi]])
                b0 = Q * 16 + q4 * 4
                s.add("dve" if q4 % 2 == 0 else "act",
                      (lambda e, pst=pst, b0=b0: e.tensor_copy(out=vs[:, b0:b0 + 4, :], in_=pst[:, 0:512].rearrange("p (j f) -> p j f", j=4)))
                      if q4 % 2 == 0 else
                      (lambda e, pst=pst, b0=b0: e.activation(out=vs[:, b0:b0 + 4, :], in_=pst[:, 0:512].rearrange("p (j f) -> p j f", j=4), func=AF.Copy)),
                      reads=[cx.psr[pi]], writes=[vs_r[Q]])
        for qt in range(NQT):
            Qq = qt // 4
            qc0 = qt * 512
            kblocks = [(kb, None) for kb in range(4 * qt)] + [(4 * qt + j, j) for j in range(4)]
            nk = len(kblocks)

            def emit_S(idx, i=i, qt=qt, Qq=Qq, qc0=qc0, kblocks=kblocks):
                nonlocal nS
                kb, dj = kblocks[idx]
                Qk = kb // 16
                c0 = 0 if dj is None else 128 * dj
                mi = 0 if dj is None else 1 + dj
                nrel = (4 * qt - kb) if dj is None else 0
                res = []
                for c in range(2):
                    pi = nS % 4
                    si = nS % NSB
                    pslot = nS % NP
                    nS += 1
                    rows = slice(64 * c, 64 * c + 64)
                    s.add("pe", lambda e, pi=pi, rows=rows, kb=kb, c0=c0: e.matmul(
                        cx.ps[pi][:, c0:512], ks[rows, kb * 128:(kb + 1) * 128], qs[rows, qc0 + c0:qc0 + 512],
                        start=True, stop=True), reads=[ks_r[Qk], qs_r[Qq]], writes=[cx.psr[pi]])
                    s.add("dve", lambda e, si=si, pi=pi, mi=mi, c0=c0: e.scalar_tensor_tensor(
                        out=sbb[:, si, c0:512], in0=mtab[:, mi, c0:512], scalar=cst[:, i:i + 1], in1=cx.ps[pi][:, c0:512],
                        op0=ALU.mult, op1=ALU.add), reads=[mtab_r, cst_r, cx.psr[pi]], writes=[sbb_r[si]])
                    bc = 2 + i * 64 + nrel
                    s.add("act", lambda e, si=si, pslot=pslot, c0=c0, bc=bc: e.activation(
                        out=pb[:, pslot, c0:512], in_=sbb[:, si, c0:512], func=AF.Exp, scale=0.125, bias=cst[:, bc:bc + 1]),
                        reads=[sbb_r[si], cst_r], writes=[pb_r[pslot]])
                    res.append((pslot, c0))
                return res

            def emit_PV(idx, pinfo, i=i, kblocks=kblocks, nk=nk):
                kb, dj = kblocks[idx]
                Qk = kb // 16
                for c in range(2):
                    pslot, c0 = pinfo[c]
                    s.add("pe", lambda e, c=c, kb=kb, pslot=pslot, c0=c0, idx=idx: e.matmul(
                        cx.ps[4 + c][:, c0:512], vs[:, kb, :], pb[:, pslot, c0:512],
                        start=(idx == 0), stop=(idx == nk - 1)),
                        reads=[vs_r[Qk], pb_r[pslot]], writes=[cx.psr[4 + c]])
                    s.add("pe", lambda e, c=c, pslot=pslot, c0=c0, idx=idx: e.matmul(
                        cx.ps[6 + c][:, c0:512], ones_b[:, :], pb[:, pslot, c0:512],
                        start=(idx == 0), stop=(idx == nk - 1)),
                        reads=[ones_b_r, pb_r[pslot]], writes=[cx.psr[6 + c]])

            pin = emit_S(0)
            for idx in range(nk):
                nxt = emit_S(idx + 1) if idx + 1 < nk else None
                emit_PV(idx, pin)
                pin = nxt
            for c in range(2):
                s.add("dve", lambda e, c=c: e.reciprocal(out=rd[:, c, :], in_=cx.ps[6 + c][:, :]),
                      reads=[cx.psr[6 + c]], writes=[rd_r[c]])
                s.add("dve", lambda e, c=c: e.tensor_tensor(out=tt_[:, c, :], in0=cx.ps[4 + c][:, :], in1=rd[:, c, :], op=ALU.mult),
                      reads=[cx.psr[4 + c], rd_r[c]], writes=[tt_r[c]])
            s.add("dve", lambda e: e.scalar_tensor_tensor(out=ob[:, :], in0=tt_[:, 1, :], scalar=neglam[:, 0:1], in1=tt_[:, 0, :],
                                                          op0=ALU.mult, op1=ALU.add),
                  reads=[tt_r[0], tt_r[1], neglam_r], writes=[ob_r])
            emit_rstd(cx, lambda c: ob[:, :], 1, 512, rstd, rstd_r, lambda c: [ob_r], sq, sq_r, (nS % 4), inv_n=1.0 / 128)
            s.add("dve", lambda e: e.scalar_tensor_tensor(out=ob[:, :], in0=ob[:, :], scalar=sub[:, 0:1], in1=rstd[:, :],
                                                          op0=ALU.mult, op1=ALU.mult),
                  reads=[ob_r, sub_r, rstd_r], writes=[ob_r])
            k_ = qt % 2
            s.add("act", lambda e, k_=k_: e.activation(out=ost[:, k_, :], in_=ob[:, :], func=AF.Copy, scale=float(1.0 - lambda_init)),
                  reads=[ob_r], writes=[ost_r[k_]])
            r_ = Res()
            s.add("pool", lambda e, i=i, qt=qt, k_=k_: e.dma_start(out=o_dst(i, qt), in_=ost[:, k_, :]),
                  reads=[ost_r[k_]], writes=[r_], dma=True)
            out_rs.append(r_)
    return out_rs


def emit_outproj(cx, x, x_r, o_my, o_my_r, w_ap, gcol, g_r):
    s = cx.s
    cx.phase()
    T = TPC
    NT = T // 512
    w = cx.ar("wo", [128, NCH, D], BF16)
    w_r = Res()
    s.add("pool", lambda e: e.dma_start(out=w[:, :, :], in_=w_ap.rearrange("(c p) f -> p c f", p=128)),
          writes=[w_r], dma=True)
    o = cx.ar("oo", [128, NCH, T], BF16)
    o_r = [Res() for _ in range(NCH)]
    for c in range(NCH):
        s.add("sp", lambda e, c=c: e.dma_start(out=o[:, c, :], in_=o_my.ap()[c * 128:(c + 1) * 128, :]),
              reads=[o_my_r], writes=[o_r[c]], dma=True)
    fT = cx.ar("fTo", [128, NCH, 512], F32)
    fT_r = [Res() for _ in range(NCH)]
    sq = cx.ar("sq", [128, 2, 512], F32)
    sq_r = [Res(), Res()]
    rstd = cx.ar("rstd", [128, 512], F32)
    rstd_r = Res()
    tmp = cx.ar("tmp", [128, 2, 512], F32)
    tmp_r = [Res(), Res()]
    n = 0
    for t in range(NT):
        cs = slice(t * 512, (t + 1) * 512)
        for c in range(NCH):
            pi = n % 4
            n += 1
            for k in range(NCH):
                s.add("pe", lambda e, c=c, k=k, cs=cs, pi=pi: e.matmul(
                    cx.ps[pi][:, :], w[:, k, c * 128:(c + 1) * 128], o[:, k, cs],
                    start=(k == 0), stop=(k == NCH - 1)),
                    reads=[w_r, o_r[k]], writes=[cx.psr[pi]])
            s.add("act", lambda e, c=c, pi=pi: e.activation(out=fT[:, c, :], in_=cx.ps[pi][:, :], func=AF.Copy),
                  reads=[cx.psr[pi]], writes=[fT_r[c]])
        emit_rstd(cx, lambda c: fT[:, c, :], NCH, 512, rstd, rstd_r, lambda c: [fT_r[c]], sq, sq_r, 6 + (t % 2))
        for c in range(NCH):
            k = c % 2
            s.add("dve", lambda e, c=c, k=k: e.scalar_tensor_tensor(
                out=tmp[:, k, :], in0=fT[:, c, :], scalar=gcol(c), in1=rstd[:, :],
                op0=ALU.mult, op1=ALU.mult),
                reads=[fT_r[c], g_r, rstd_r], writes=[tmp_r[k]])
            s.add("pool", lambda e, c=c, cs=cs, k=k: e.tensor_tensor(
                out=x[:, c, cs], in0=x[:, c, cs], in1=tmp[:, k, :], op=ALU.add),
                reads=[tmp_r[k]], writes=[x_r[c][t]])


GROUPS4 = [[0, 1, 2, 3], [4, 5, 6, 7]]
LAYER_KINDS = (0, 1, 2, 0)
A_DILS = (1, 4, 16)


def emit_gather_pieces(cx, snd, snd_rs, rcv, npieces):
    outs = []
    for i in range(npieces):
        r_ = Res()
        cx.s.add_cc(lambda e, i=i: e.collective_compute(
            "AllGather", ALU.bypass, replica_groups=GROUPS4,
            ins=[snd.ap()[i * 128:(i + 1) * 128, :]], outs=[rcv.ap()[i * 512:(i + 1) * 512, :]]),
            reads=snd_rs[i], writes=[r_])
        outs.append(r_)
    return outs


def emit_hn_pieces(cx, x, x_r, gc, g_r, hn_snd):
    s_ = cx.s
    NT = TPC // 512
    cx.phase()
    sq = cx.ar("sq", [128, 2, 512], F32)
    rstd = cx.ar("rstd", [128, 512], F32)
    hst = cx.ar("hst", [128, 2, NCH, 512], BF16)
    sq_r, rstd_r = [Res(), Res()], Res()
    hst_r = [[Res() for _ in range(NCH)] for _ in range(2)]
    snd_rs = [[] for _ in range(NCH)]
    for t in range(NT):
        k = t % 2
        cs = slice(t * 512, (t + 1) * 512)
        emit_rstd(cx, lambda c, cs=cs: x[:, c, cs], NCH, 512, rstd, rstd_r,
                  lambda c, t=t: [x_r[c][t]], sq, sq_r, 6 + k)
        for c in range(NCH):
            s_.add("dve", lambda e, c=c, cs=cs, k=k: e.scalar_tensor_tensor(
                out=hst[:, k, c, :], in0=x[:, c, cs], scalar=gc(c), in1=rstd[:, :],
                op0=ALU.mult, op1=ALU.mult),
                reads=[x_r[c][t], g_r, rstd_r], writes=[hst_r[k][c]])
            r_ = Res()
            s_.add("sp", lambda e, c=c, cs=cs, k=k: e.dma_start(out=hn_snd.ap()[c * 128:(c + 1) * 128, cs], in_=hst[:, k, c, :]),
                   reads=[hst_r[k][c]], writes=[r_], dma=True)
            snd_rs[c].append(r_)
    return snd_rs


def build_fused(nlayers=4, layers=None, dbg=False):
    cx = Ctx()
    nc, s = cx.nc, cx.s
    T = TPC
    NT = T // 512
    xT_d = cx.dram_in("xT", [D, T])
    gpre_d = cx.dram_in("gpre", [128, 96])
    gpost_d = cx.dram_in("gpost", [128, 96])
    wg_d = cx.dram_in("ffn_wg", [4, 2, D, DFF])
    wu_d = cx.dram_in("ffn_wu", [4, 2, D, DFF])
    wd_d = cx.dram_in("ffn_wd", [4, 2, DFF, D])
    a_win_d = cx.dram_in("a_win", [2, D, 18 * 128])
    a_wout_d = cx.dram_in("a_wout", [2, D, D])
    b_win_d = cx.dram_in("b_win", [D, 6 * 128])
    b_wout_d = cx.dram_in("b_wout", [D, D])
    c_win_d = cx.dram_in("c_win", [D, 4 * 128])
    c_b_d = cx.dram_in("c_b", [128, 4])
    c_wout_d = cx.dram_in("c_wout", [D, D])
    dnegA_d = cx.dram_in("dnegA", [128, 256])
    dnegC_d = cx.dram_in("dnegC", [128, 256])
    cstA_d = cx.dram_in("cstA", [128, 12])
    cstC_d = cx.dram_in("cstC", [128, 4])
    sinkC_d = cx.dram_in("sinkC", [128, 2])
    mtab_d = cx.dram_in("mtab", [128, 5, 512])
    cstB_d = cx.dram_in("cstB", [128, 130])
    lam_d = cx.dram_in("lam", [128, 4, 64])
    sub_d = cx.dram_in("sub", [128, 1])
    ident_d = cx.dram_in("ident", [128, 128], BF16)
    out_d = cx.dram_out("yT", [D, T])
    hn_snd = cx.dram("hn_snd", [NCH * 128, T], BF16)
    hn_all = cx.dram("hn_all", [NCH * 512, T], BF16)
    scratch = cx.dram("scratch", [18 * 128, SEQ], BF16)
    o_snd = cx.dram("o_snd", [8 * 128, T], BF16)
    o_all = cx.dram("o_all", [8 * 512, T], BF16)
    x = cx.perm("x", [128, NCH, T], F32)
    x_r = [[Res() for _ in range(NT)] for _ in range(NCH)]
    gpre = cx.perm("gpre_s", [128, 96], F32)
    gpost = cx.perm("gpost_s", [128, 96], F32)
    gposth = cx.perm("gposth_s", [128, 96], F32)
    ident = cx.perm("ident_s", [128, 128], BF16)
    g_r, ident_r = Res(), Res()
    cx.arena_start()
    s.add("sp", lambda e: e.dma_start(out=gpre[:, :], in_=gpre_d.ap()), writes=[g_r], dma=True)
    s.add("sp", lambda e: e.dma_start(out=gpost[:, :], in_=gpost_d.ap()), writes=[g_r], dma=True)
    s.add("sp", lambda e: e.dma_start(out=ident[:, :], in_=ident_d.ap()), writes=[ident_r], dma=True)
    s.add("dve", lambda e: e.tensor_scalar(out=gposth[:, :], in0=gpost[:, :], scalar1=0.5, scalar2=None, op0=ALU.mult),
          reads=[g_r], writes=[g_r])
    for c in range(NCH):
        for t in range(NT):
            s.add("sp", lambda e, c=c, t=t: e.dma_start(out=x[:, c, t * 512:(t + 1) * 512],
                                                        in_=xT_d.ap()[c * 128:(c + 1) * 128, t * 512:(t + 1) * 512]),
                  writes=[x_r[c][t]], dma=True)

    def gcol(tab, i, sl):
        return lambda c: tab[:, (i * 3 + sl) * 8 + c:(i * 3 + sl) * 8 + c + 1]

    for i in (layers if layers is not None else range(nlayers)):
        kind, j = LAYER_KINDS[i], i // 3
        cx.phase()
        cx.s.new_epoch()
        emit_ffn(cx, x, x_r, wg_d.ap()[i, 0], wu_d.ap()[i, 0], wd_d.ap()[i, 0], gcol(gpre, i, 0), gcol(gposth, i, 0), g_r)
        if dbg:
            dump_x(cx, x, x_r, "dbg%d_a" % i)
        snd_rs = emit_hn_pieces(cx, x, x_r, gcol(gpre, i, 1), g_r, hn_snd)
        hn_rs = emit_gather_pieces(cx, hn_snd, snd_rs, hn_all, NCH)
        if kind == 0:
            nblk, w_ap, bias_ap = 18, a_win_d.ap()[j], None
            dil_of_blk = [A_DILS[(b_ // 3) % 3] for b_ in range(18)]
        elif kind == 1:
            nblk, w_ap, bias_ap = 6, b_win_d.ap(), None
            dil_of_blk = [1] * 6
        else:
            nblk, w_ap, bias_ap = 4, c_win_d.ap(), c_b_d.ap()
            dil_of_blk = [1] * 4
        blk_r = emit_proj_heads(cx, hn_all, hn_rs, w_ap, nblk, dil_of_blk, scratch, bias_ap)
        o_dst = lambda pr, Q: o_snd.ap()[(Q * 2 + pr) * 128:(Q * 2 + pr + 1) * 128, :]
        if kind == 0:
            qblk = [[(pr * 3 + g) * 3 + 0 for g in range(3)] for pr in range(2)]
            kblk = [[(pr * 3 + g) * 3 + 1 for g in range(3)] for pr in range(2)]
            vblk = [[(pr * 3 + g) * 3 + 2 for g in range(3)] for pr in range(2)]
            outs = emit_banded(cx, scratch, blk_r, qblk, kblk, vblk, A_DILS, dnegA_d.ap(), cstA_d.ap(), None, ident, ident_r, o_dst)
            piece_rs = [[outs[pr * 4 + Q]] for Q in range(4) for pr in range(2)]
        elif kind == 2:
            outs = emit_banded(cx, scratch, blk_r, [[0], [1]], [[2], [2]], [[3], [3]], (1,), dnegC_d.ap(), cstC_d.ap(), sinkC_d.ap(),
                               ident, ident_r, o_dst)
            piece_rs = [[outs[pr * 4 + Q]] for Q in range(4) for pr in range(2)]
        else:
            li = 0.8 - 0.6 * math.exp(-0.3 * i)
            o_dst_b = lambda ih, qt: o_snd.ap()[((qt // 4) * 2 + ih) * 128:((qt // 4) * 2 + ih + 1) * 128, (qt % 4) * 512:(qt % 4 + 1) * 512]
            outs = emit_diff(cx, scratch, blk_r, li, mtab_d.ap(), cstB_d.ap(), lam_d.ap(), sub_d.ap(), ident, ident_r, o_dst_b)
            piece_rs = [[outs[ih * 16 + Q * 4 + k_] for k_ in range(4)] for Q in range(4) for ih in range(2)]
        o_rs = emit_gather_pieces(cx, o_snd, piece_rs, o_all, 8)
        w_out_ap = (a_wout_d.ap()[j] if kind == 0 else (b_wout_d.ap() if kind == 1 else c_wout_d.ap()))
        emit_outproj_dyn(cx, x, x_r, o_all, o_rs, w_out_ap, gcol(gpost, i, 1), g_r)
        if dbg:
            dump_x(cx, x, x_r, "dbg%d_b" % i)
        emit_ffn(cx, x, x_r, wg_d.ap()[i, 1], wu_d.ap()[i, 1], wd_d.ap()[i, 1], gcol(gpre, i, 2), gcol(gposth, i, 2), g_r)
    cx.phase()
    for c in range(NCH):
        for t in range(NT):
            s.add("sp", lambda e, c=c, t=t: e.dma_start(out=out_d.ap()[c * 128:(c + 1) * 128, t * 512:(t + 1) * 512],
                                                        in_=x[:, c, t * 512:(t + 1) * 512]),
                  reads=[x_r[c][t]], dma=True)
    s.emit()
    return nc


def dump_x(cx, x, x_r, name):
    cx.phase()
    d = cx.dram_out(name, [D, TPC])
    for c in range(NCH):
        for t in range(TPC // 512):
            cx.s.add("sp", lambda e, c=c, t=t: e.dma_start(out=d.ap()[c * 128:(c + 1) * 128, t * 512:(t + 1) * 512],
                                                           in_=x[:, c, t * 512:(t + 1) * 512]),
                     reads=[x_r[c][t]], dma=True)


def emit_outproj_dyn(cx, x, x_r, o_all, o_rs, w_ap, gcol, g_r):
    s = cx.s
    s.want_pid = True
    cx.phase()
    T = TPC
    NT = T // 512
    w = cx.ar("wo", [128, NCH, D], BF16)
    w_r = Res()
    s.add("pool", lambda e: e.dma_start(out=w[:, :, :], in_=w_ap.rearrange("(c p) f -> p c f", p=128)),
          writes=[w_r], dma=True)
    o = cx.ar("oo", [128, NCH, T], BF16)
    o_r = [Res() for _ in range(NCH)]
    for hq in range(4):
        for pr in range(2):
            c = hq * 2 + pr

            def fn(e, hq=hq, pr=pr, c=c):
                rank = cx.s.pid % 4
                return e.dma_start(out=o[:, c, :], in_=o_all.ap()[bass.ds(rank * 1024 + pr * 512 + hq * 128, 128), :])
            s.add("pool", fn, reads=o_rs, writes=[o_r[c]], dma=True)
    fT = cx.ar("fTo", [128, NCH, 512], F32)
    fT_r = [Res() for _ in range(NCH)]
    sq = cx.ar("sq", [128, 2, 512], F32)
    sq_r = [Res(), Res()]
    rstd = cx.ar("rstd", [128, 512], F32)
    rstd_r = Res()
    tmp = cx.ar("tmp", [128, 2, 512], F32)
    tmp_r = [Res(), Res()]
    n = 0
    for t in range(NT):
        cs = slice(t * 512, (t + 1) * 512)
        for c in range(NCH):
            pi = n % 4
            n += 1
            for k in range(NCH):
                s.add("pe", lambda e, c=c, k=k, cs=cs, pi=pi: e.matmul(
                    cx.ps[pi][:, :], w[:, k, c * 128:(c + 1) * 128], o[:, k, cs],
                    start=(k == 0), stop=(k == NCH - 1)),
                    reads=[w_r, o_r[k]], writes=[cx.psr[pi]])
            s.add("act", lambda e, c=c, pi=pi: e.activation(out=fT[:, c, :], in_=cx.ps[pi][:, :], func=AF.Copy),
                  reads=[cx.psr[pi]], writes=[fT_r[c]])
        emit_rstd(cx, lambda c: fT[:, c, :], NCH, 512, rstd, rstd_r, lambda c: [fT_r[c]], sq, sq_r, 6 + (t % 2))
        for c in range(NCH):
            k = c % 2
            s.add("dve", lambda e, c=c, k=k: e.scalar_tensor_tensor(
                out=tmp[:, k, :], in0=fT[:, c, :], scalar=gcol(c), in1=rstd[:, :],
                op0=ALU.mult, op1=ALU.mult),
                reads=[fT_r[c], g_r, rstd_r], writes=[tmp_r[k]])
            s.add("pool", lambda e, c=c, cs=cs, k=k: e.tensor_tensor(
                out=x[:, c, cs], in0=x[:, c, cs], in1=tmp[:, k, :], op=ALU.add),
                reads=[tmp_r[k]], writes=[x_r[c][t]])


def alibi_slopes(n):
    return np.array([2.0 ** (-8.0 * (h + 1) / n) for h in range(n)], dtype=np.float64)


MASKV = -1.0e30


def dneg_table(max_dist):
    k = np.arange(128)[:, None]
    q = np.arange(128)[None, :]
    cur = np.where(q - k >= 0, -(q - k).astype(np.float64), MASKV)
    dn = q + 128 - k
    nxt = np.where(dn <= max_dist, -dn.astype(np.float64), MASKV)
    return np.ascontiguousarray(np.concatenate([cur, nxt], axis=1).astype(np.float32))


def diff_mtab():
    k = np.arange(128)[:, None].astype(np.float64)
    q = np.arange(512)[None, :].astype(np.float64)
    tabs = [-(q - k)]
    for j in range(4):
        val = q - 128 * j - k
        tabs.append(np.where(val >= 0, -val, MASKV))
    return np.ascontiguousarray(np.stack(tabs, axis=1).astype(np.float32))


def _f32(a):
    return np.ascontiguousarray(np.asarray(a, dtype=np.float32))


def _gains(g):
    g = np.asarray(g, dtype=np.float32)
    return np.ascontiguousarray(g.reshape(4, 3, 8, 128).transpose(3, 0, 1, 2).reshape(128, 96))


_NC_CACHE = {}


def make_in_maps(x, norm_pre, norm_post, ffn_w_gate, ffn_w_up, ffn_w_down, a_w_in, a_w_out, b_w_in, b_w_out,
                 b_lambda, b_subln, c_w_in, c_b_in, c_w_out, c_sinks):
    x = np.asarray(x, dtype=np.float32)
    a_w_in = np.asarray(a_w_in, dtype=np.float32)
    b_w_in = np.asarray(b_w_in, dtype=np.float32)
    c_w_in = np.asarray(c_w_in, dtype=np.float32)
    c_b_in = np.asarray(c_b_in, dtype=np.float32)
    sinks = np.asarray(c_sinks, dtype=np.float32)[0]
    common = dict(
        gpre=_gains(norm_pre), gpost=_gains(norm_post),
        ffn_wg=_f32(ffn_w_gate), ffn_wu=_f32(ffn_w_up), ffn_wd=_f32(ffn_w_down),
        a_wout=_f32(a_w_out), b_wout=_f32(np.asarray(b_w_out)[0]), c_wout=_f32(np.asarray(c_w_out)[0]),
        dnegA=dneg_table(128), dnegC=dneg_table(127), mtab=diff_mtab(),
        lam=np.ascontiguousarray(np.broadcast_to(np.asarray(b_lambda, dtype=np.float32)[0][None], (128, 4, 64))),
        sub=_f32(np.asarray(b_subln)[0]).reshape(128, 1),
        ident=np.eye(128, dtype=np.float32).astype(NPBF16),
    )
    sl16, sl8 = alibi_slopes(16), alibi_slopes(8)
    maps = []
    for c in range(NCORES):
        b_, hq = c // 4, c % 4
        m = dict(common)
        m["xT"] = np.ascontiguousarray(x[b_, hq * TPC:(hq + 1) * TPC, :].T)
        cols = []
        cstA = np.zeros((128, 12), np.float32)
        for pr in range(2):
            h0 = 4 * hq + 2 * pr
            for g in range(3):
                for cc in range(3):
                    base = ((g * 3 + cc) * 16 + h0) * 64
                    cols.append(np.arange(base, base + 128))
                for e in range(2):
                    cstA[:, (pr * 3 + g) * 2 + e] = 8.0 * sl16[h0 + e] * A_DILS[g]
        cols = np.concatenate(cols)
        m["a_win"] = np.ascontiguousarray(a_w_in[:, :, cols])
        m["cstA"] = cstA
        cols = []
        cstB = np.zeros((128, 130), np.float32)
        for i in range(2):
            h = 2 * hq + i
            for cc in range(3):
                cols.append(np.arange(cc * 1024 + h * 128, cc * 1024 + (h + 1) * 128))
            cstB[:, i] = 8.0 * sl8[h]
            cstB[:, 2 + i * 64:2 + (i + 1) * 64] = (-sl8[h] * 128.0 * np.arange(64))[None, :]
        cols = np.concatenate(cols)
        m["b_win"] = np.ascontiguousarray(b_w_in[0][:, cols])
        m["cstB"] = cstB
        kv = hq // 2
        cols = np.concatenate([np.arange(4 * hq * 64, 4 * hq * 64 + 256),
                               np.arange(1024 + kv * 64, 1024 + kv * 64 + 64), np.arange(1024 + kv * 64, 1024 + kv * 64 + 64),
                               np.arange(1152 + kv * 64, 1152 + kv * 64 + 64), np.arange(1152 + kv * 64, 1152 + kv * 64 + 64)])
        m["c_win"] = np.ascontiguousarray(c_w_in[0][:, cols])
        m["c_b"] = np.ascontiguousarray(c_b_in[0][cols].reshape(4, 128).T)
        cstC = np.zeros((128, 4), np.float32)
        sk = np.zeros((128, 2), np.float32)
        for pr in range(2):
            h0 = 4 * hq + 2 * pr
            for e in range(2):
                cstC[:, pr * 2 + e] = 8.0 * sl16[h0 + e]
                sk[64 * e:64 * e + 64, pr] = sinks[h0 + e]
        m["cstC"] = cstC
        m["sinkC"] = sk
        maps.append(m)
    return maps


def kernel(**inputs):
    if "nc" not in _NC_CACHE:
        _NC_CACHE["nc"] = build_fused(4)
    maps = make_in_maps(**inputs)
    res = run_bass_kernel_spmd(_NC_CACHE["nc"], maps, core_ids=list(range(NCORES)))
    out = np.empty((BATCH, SEQ, D), np.float32)
    for c in range(NCORES):
        b_, ch = c // 4, c % 4
        out[b_, ch * TPC:(ch + 1) * TPC, :] = np.asarray(res.results[c]["yT"]).T
    return out
```

```python
import math
import numpy as np
import ml_dtypes
import concourse.bass as bass
import concourse.mybir as mybir
from concourse.bass_utils import run_bass_kernel_spmd

F32 = mybir.dt.float32
BF16 = mybir.dt.bfloat16
AF = mybir.ActivationFunctionType
ALU = mybir.AluOpType
NPBF16 = ml_dtypes.bfloat16

D = 1024
DFF = 2816
NCH = D // 128
NFC = DFF // 128
EPS = 1e-6
NCORES = 8
SEQ = 8192
BATCH = 2
TPC = 2048


class Res:
    __slots__ = ("w", "r", "name")

    def __init__(self, name=""):
        self.w = None
        self.r = []
        self.name = name


class Sched:
    ENGS = ("pe", "act", "dve", "pool", "sp")
    NDMASEM = 24

    def __init__(self, nc):
        self.nc = nc
        self.ops = {e: [] for e in self.ENGS}
        self.dma_count = [0] * self.NDMASEM
        self.dma_last = [None] * self.NDMASEM
        self.ndma = 0
        self.epoch = 0

    def new_epoch(self):
        self.epoch += 1

    def res(self, name=""):
        return Res(name)

    def barrier(self):
        lasts = []
        for e in self.ENGS:
            for op in reversed(self.ops[e]):
                if op["fn"] is not None:
                    lasts.append(op)
                    break
        lasts += [d for d in self.dma_last if d is not None]
        if getattr(self, "cc_last", None) is not None:
            lasts.append(self.cc_last)
        for e in self.ENGS:
            self.add(e, None, extra_deps=lasts)

    def add_cc(self, fn, reads=(), writes=()):
        op = self.add("pool", fn, reads=reads, writes=writes)
        self.cc_count = getattr(self, "cc_count", 0) + 1
        op["cc"] = self.cc_count
        self.cc_last = op
        return op

    def add(self, eng, fn, reads=(), writes=(), dma=False, extra_deps=()):
        deps = list(extra_deps)
        for r in reads:
            if r.w is not None:
                deps.append(r.w)
        for r in writes:
            if r.w is not None:
                deps.append(r.w)
            deps.extend(r.r)
        op = {"eng": eng, "fn": fn, "deps": deps, "signal": False, "dma": None, "cc": None, "idx": len(self.ops[eng]), "epoch": self.epoch}
        if dma:
            s = self.ndma % self.NDMASEM
            self.ndma += 1
            if self.dma_last[s] is not None:
                op["deps"].append(self.dma_last[s])
            self.dma_count[s] += 1
            op["dma"] = (s, 16 * self.dma_count[s])
            self.dma_last[s] = op
        self.ops[eng].append(op)
        for r in reads:
            r.r.append(op)
        for r in writes:
            r.w = op
            r.r = []
        return op

    def emit(self):
        nc = self.nc
        for e in self.ENGS:
            for op in self.ops[e]:
                ded = []
                seen = set()
                best = {}
                for d in op["deps"]:
                    if id(d) in seen:
                        continue
                    seen.add(id(d))
                    if d["dma"] is None and d["cc"] is None:
                        if d["eng"] == "pe" and e == "pe":
                            continue
                        if d["epoch"] < op["epoch"]:
                            continue
                        if d["eng"] not in best or d["idx"] > best[d["eng"]]["idx"]:
                            best[d["eng"]] = d
                    else:
                        ded.append(d)
                for d in best.values():
                    d["signal"] = True
                    ded.append(d)
                op["deps"] = ded
        for e in self.ENGS:
            c = 0
            ep = 0
            for op in self.ops[e]:
                if op["epoch"] != ep:
                    ep = op["epoch"]
                    c = 0
                if op["signal"]:
                    c += 1
                op["cnt"] = c
        import contextlib
        with contextlib.ExitStack() as st:
            esem = {(ep, e): st.enter_context(nc.semaphore("s_%s_%d" % (e, ep))) for e in self.ENGS for ep in range(self.epoch + 1)}
            dsem = [st.enter_context(nc.semaphore("d_%d" % i)) for i in range(self.NDMASEM)]
            ccsem = st.enter_context(nc.semaphore("s_cc"))
            block = st.enter_context(nc.Block())

            def run(e, eng):
                waited = {}
                for op in self.ops[e]:
                    need = {}
                    for d in op["deps"]:
                        if d["dma"] is not None:
                            key, val = ("d", d["dma"][0]), d["dma"][1]
                        elif d["cc"] is not None:
                            key, val = ("c", 0), d["cc"]
                        else:
                            key, val = ("e", (d["epoch"], d["eng"])), d["cnt"]
                        if val > waited.get(key, 0) and val > need.get(key, 0):
                            need[key] = val
                    for key, val in need.items():
                        sem = dsem[key[1]] if key[0] == "d" else (ccsem if key[0] == "c" else esem[key[1]])
                        eng.wait_ge(sem, val)
                        waited[key] = val
                    if op["fn"] is None:
                        continue
                    ins = op["fn"](eng)
                    if op["cc"] is not None:
                        ins.then_inc(ccsem, 1)
                    elif op["dma"] is not None:
                        ins.then_inc(dsem[op["dma"][0]], 16)
                    elif op["signal"]:
                        ins.then_inc(esem[(op["epoch"], e)], 1)
                if e == "sp":
                    for s in range(self.NDMASEM):
                        if self.dma_count[s] > 0:
                            eng.wait_ge(dsem[s], 16 * self.dma_count[s])
                    if getattr(self, "cc_count", 0) > 0:
                        eng.wait_ge(ccsem, self.cc_count)

            @block.tensor
            def _(eng):
                run("pe", eng)

            @block.scalar
            def _(eng):
                run("act", eng)

            @block.vector
            def _(eng):
                run("dve", eng)

            @block.gpsimd
            def _(eng):
                if getattr(self, "want_pid", False):
                    self.pid = eng.partition_id()
                run("pool", eng)

            @block.sync
            def _(eng):
                run("sp", eng)


class Ctx:
    def __init__(self):
        self.nc = bass.Bass("TRN2", target_bir_lowering=False)
        self.s = Sched(self.nc)
        self.n = 0
        self.uid = 0
        nc = self.nc
        self.ps = [nc.alloc_psum_tensor("ps%d" % i, [128, 512], F32) for i in range(8)]
        self.psr = [Res("ps%d" % i) for i in range(8)]
        self.ones_f = nc.alloc_sbuf_tensor("ones_f", [128, 128], F32)
        self.ones_r = Res("ones_f")
        self.s.add("pool", lambda e: e.memset(self.ones_f[:, :], 1.0), writes=[self.ones_r])
        self.eps_col = nc.alloc_sbuf_tensor("eps", [128, 1], F32)
        self.eps_r = Res()
        self.s.add("pool", lambda e: e.memset(self.eps_col[:, :], EPS), writes=[self.eps_r])
        self.abase = None

    def perm(self, name, shape, dt):
        assert self.abase is None
        return self.nc.alloc_sbuf_tensor(name, shape, dt)

    def arena_start(self):
        self.abase = (self.nc.sbuf_base + 63) // 64 * 64
        self.atop = self.nc.sbuf_top
        self.apos = self.abase

    def phase(self):
        self.s.barrier()
        self.apos = self.abase
        self.psr = [Res("ps%d" % i) for i in range(8)]

    def ar(self, name, shape, dt):
        sz = 4 if dt == F32 else 2
        nb = sz
        for d_ in shape[1:]:
            nb *= d_
        nb = (nb + 63) // 64 * 64
        self.uid += 1
        t = self.nc.alloc_sbuf_tensor_at("%s_%d" % (name, self.uid), list(shape), dt, offset=self.apos)
        self.apos += nb
        assert self.apos <= self.atop, ("arena overflow", name, self.apos, self.atop)
        return t

    def dram_in(self, name, shape, dt=F32):
        return self.nc.dram_tensor(name, list(shape), dt, kind="ExternalInput")

    def dram_out(self, name, shape, dt=F32):
        return self.nc.dram_tensor(name, list(shape), dt, kind="ExternalOutput")

    def dram(self, name, shape, dt):
        return self.nc.dram_tensor(name, list(shape), dt)


def emit_rstd(cx, src_fn, nsrc, ncols, rstd, rstd_r, src_reads, sq, sq_r, psi, inv_n=1.0 / D):
    s = cx.s
    ps, psr = cx.ps[psi], cx.psr[psi]
    for c in range(nsrc):
        s.add("act", lambda e, c=c: e.activation(out=sq[:, c % 2, :ncols], in_=src_fn(c), func=AF.Square),
              reads=src_reads(c), writes=[sq_r[c % 2]])
        s.add("pe", lambda e, c=c: e.matmul(ps[:, :ncols], cx.ones_f[:, :], sq[:, c % 2, :ncols],
                                            start=(c == 0), stop=(c == nsrc - 1)),
              reads=[sq_r[c % 2], cx.ones_r], writes=[psr])
    s.add("act", lambda e: e.activation(out=rstd[:, :ncols], in_=ps[:, :ncols], func=AF.Sqrt,
                                        scale=inv_n, bias=cx.eps_col[:, 0:1]),
          reads=[psr, cx.eps_r], writes=[rstd_r])
    s.add("dve", lambda e: e.reciprocal(out=rstd[:, :ncols], in_=rstd[:, :ncols]),
          reads=[rstd_r], writes=[rstd_r])


def emit_ffn(cx, x, x_r, wg_ap, wu_ap, wd_ap, g1col, g2col, g_r):
    s = cx.s
    T, HALF = TPC, 1024
    cx.phase()
    xn = cx.ar("xn", [128, NCH, HALF], BF16)
    hT = cx.ar("hT", [128, NFC, HALF], BF16)
    fT = cx.ar("fT", [128, NCH, HALF], F32)
    sq = cx.ar("sq", [128, 2, 512], F32)
    rstd = cx.ar("rstd", [128, 512], F32)
    sg = cx.ar("sg", [128, 2, 512], F32)
    tmp = cx.ar("tmp", [128, 2, 512], F32)
    NWS, NDS = 2, 3
    wgs = cx.ar("wgs", [128, NWS, NCH, 256], BF16)
    wus = cx.ar("wus", [128, NWS, NCH, 256], BF16)
    wds = cx.ar("wds", [128, NDS, NFC, 128], BF16)
    xn_r = [[Res() for _ in range(2)] for _ in range(NCH)]
    hT_r = [[Res() for _ in range(2)] for _ in range(NFC)]
    fT_r = [[Res() for _ in range(2)] for _ in range(NCH)]
    sq_r, sg_r, tmp_r = [Res(), Res()], [Res(), Res()], [Res(), Res()]
    rstd_r = Res()
    wgs_r = [Res() for _ in range(NWS)]
    wus_r = [Res() for _ in range(NWS)]
    wds_r = [Res() for _ in range(NDS)]
    wg_v = wg_ap.rearrange("(c p) f -> p c f", p=128)
    wu_v = wu_ap.rearrange("(c p) f -> p c f", p=128)
    wd_v = wd_ap.rearrange("(j p) d -> p j d", p=128)
    wcount = [0, 0]
    for h in range(T // HALF):
        for tt in range(2):
            t = h * 2 + tt
            cs = slice(t * 512, (t + 1) * 512)
            ls = slice(tt * 512, (tt + 1) * 512)
            emit_rstd(cx, lambda c, cs=cs: x[:, c, cs], NCH, 512, rstd, rstd_r,
                      lambda c, t=t: [x_r[c][t]], sq, sq_r, 6)
            for c in range(NCH):
                s.add("dve", lambda e, c=c, cs=cs, ls=ls: e.scalar_tensor_tensor(
                    out=xn[:, c, ls], in0=x[:, c, cs], scalar=g1col(c), in1=rstd[:, :],
                    op0=ALU.mult, op1=ALU.mult),
                    reads=[x_r[c][t], g_r, rstd_r], writes=[xn_r[c][tt]])
        for jj in range(NFC // 2):
            slot = wcount[0] % NWS
            wcount[0] += 1
            s.add("pool", lambda e, jj=jj, slot=slot: e.dma_start(out=wgs[:, slot, :, :], in_=wg_v[:, :, jj * 256:(jj + 1) * 256]),
                  writes=[wgs_r[slot]], dma=True)
            s.add("pool", lambda e, jj=jj, slot=slot: e.dma_start(out=wus[:, slot, :, :], in_=wu_v[:, :, jj * 256:(jj + 1) * 256]),
                  writes=[wus_r[slot]], dma=True)
            for tt in range(2):
                ls = slice(tt * 512, (tt + 1) * 512)
                for j2 in range(2):
                    j = jj * 2 + j2
                    k = cx.n % 2
                    cx.n += 1
                    pg, pu = 2 * k, 2 * k + 1
                    for c in range(NCH):
                        s.add("pe", lambda e, c=c, slot=slot, j2=j2, ls=ls, pg=pg: e.matmul(
                            cx.ps[pg][:, :], wgs[:, slot, c, j2 * 128:(j2 + 1) * 128], xn[:, c, ls],
                            start=(c == 0), stop=(c == NCH - 1)),
                            reads=[wgs_r[slot], xn_r[c][tt]], writes=[cx.psr[pg]])
                    for c in range(NCH):
                        s.add("pe", lambda e, c=c, slot=slot, j2=j2, ls=ls, pu=pu: e.matmul(
                            cx.ps[pu][:, :], wus[:, slot, c, j2 * 128:(j2 + 1) * 128], xn[:, c, ls],
                            start=(c == 0), stop=(c == NCH - 1)),
                            reads=[wus_r[slot], xn_r[c][tt]], writes=[cx.psr[pu]])
                    s.add("act", lambda e, k=k, pg=pg: e.activation(out=sg[:, k, :], in_=cx.ps[pg][:, :], func=AF.Silu),
                          reads=[cx.psr[pg]], writes=[sg_r[k]])
                    s.add("dve", lambda e, k=k, pu=pu, j=j, ls=ls: e.tensor_tensor(
                        out=hT[:, j, ls], in0=sg[:, k, :], in1=cx.ps[pu][:, :], op=ALU.mult),
                        reads=[sg_r[k], cx.psr[pu]], writes=[hT_r[j][tt]])
        for c in range(NCH):
            slot = wcount[1] % NDS
            wcount[1] += 1
            s.add("pool", lambda e, c=c, slot=slot: e.dma_start(out=wds[:, slot, :, :], in_=wd_v[:, :, c * 128:(c + 1) * 128]),
                  writes=[wds_r[slot]], dma=True)
            for tt in range(2):
                ls = slice(tt * 512, (tt + 1) * 512)
                pd = 4 + (cx.n % 2)
                cx.n += 1
                for j in range(NFC):
                    s.add("pe", lambda e, j=j, slot=slot, ls=ls, pd=pd: e.matmul(
                        cx.ps[pd][:, :], wds[:, slot, j, :], hT[:, j, ls],
                        start=(j == 0), stop=(j == NFC - 1)),
                        reads=[wds_r[slot], hT_r[j][tt]], writes=[cx.psr[pd]])
                s.add("act", lambda e, c=c, ls=ls, pd=pd: e.activation(out=fT[:, c, ls], in_=cx.ps[pd][:, :], func=AF.Copy),
                      reads=[cx.psr[pd]], writes=[fT_r[c][tt]])
        for tt in range(2):
            t = h * 2 + tt
            cs = slice(t * 512, (t + 1) * 512)
            ls = slice(tt * 512, (tt + 1) * 512)
            emit_rstd(cx, lambda c, ls=ls: fT[:, c, ls], NCH, 512, rstd, rstd_r,
                      lambda c, tt=tt: [fT_r[c][tt]], sq, sq_r, 7)
            for c in range(NCH):
                k = c % 2
                s.add("dve", lambda e, c=c, ls=ls, k=k: e.scalar_tensor_tensor(
                    out=tmp[:, k, :], in0=fT[:, c, ls], scalar=g2col(c), in1=rstd[:, :],
                    op0=ALU.mult, op1=ALU.mult),
                    reads=[fT_r[c][tt], g_r, rstd_r], writes=[tmp_r[k]])
                s.add("pool", lambda e, c=c, cs=cs, k=k: e.tensor_tensor(
                    out=x[:, c, cs], in0=x[:, c, cs], in1=tmp[:, k, :], op=ALU.add),
                    reads=[tmp_r[k]], writes=[x_r[c][t]])


def emit_hn_gather(cx, x, x_r, gcol, g_r, hn_snd, hn_all):
    s = cx.s
    cx.phase()
    sq = cx.ar("sq", [128, 2, 512], F32)
    rstd = cx.ar("rstd", [128, 512], F32)
    hst = cx.ar("hst", [128, 2, NCH, 512], BF16)
    sq_r, rstd_r = [Res(), Res()], Res()
    hst_r = [[Res() for _ in range(NCH)] for _ in range(2)]
    outs = []
    for t in range(TPC // 512):
        k = t % 2
        cs = slice(t * 512, (t + 1) * 512)
        emit_rstd(cx, lambda c, cs=cs: x[:, c, cs], NCH, 512, rstd, rstd_r,
                  lambda c, t=t: [x_r[c][t]], sq, sq_r, 6 + k)
        for c in range(NCH):
            s.add("dve", lambda e, c=c, cs=cs, k=k: e.scalar_tensor_tensor(
                out=hst[:, k, c, :], in0=x[:, c, cs], scalar=gcol(c), in1=rstd[:, :],
                op0=ALU.mult, op1=ALU.mult),
                reads=[x_r[c][t], g_r, rstd_r], writes=[hst_r[k][c]])
            r_ = Res()
            s.add("sp", lambda e, c=c, cs=cs, k=k: e.dma_start(out=hn_snd.ap()[c * 128:(c + 1) * 128, cs], in_=hst[:, k, c, :]),
                  reads=[hst_r[k][c]], writes=[r_], dma=True)
            outs.append(r_)
    all_r = Res()
    s.add_cc(lambda e: e.collective_compute("AllGather", ALU.bypass, replica_groups=[[0, 1, 2, 3], [4, 5, 6, 7]],
                                            ins=[hn_snd.ap()], outs=[hn_all.ap()]),
             reads=outs, writes=[all_r])
    return all_r


def emit_proj_heads(cx, hn_all, hn_all_r, w_ap, nblk, dil_of_blk, scratch, bias_ap=None):
    s = cx.s
    cx.phase()
    xn = cx.ar("xnc", [128, NCH, TPC], BF16)
    NWS, NOS = 3, 3
    ws = cx.ar("pws", [128, NWS, NCH, 256], BF16)
    ost = cx.ar("post", [128, NOS, TPC], BF16)
    xn_r = [Res() for _ in range(NCH)]
    ws_r = [Res() for _ in range(NWS)]
    ost_r = [Res() for _ in range(NOS)]
    if bias_ap is not None:
        bcol = cx.ar("bcol", [128, nblk], F32)
        b_r = Res()
        s.add("sp", lambda e: e.dma_start(out=bcol[:, :], in_=bias_ap), writes=[b_r], dma=True)
    w_v = w_ap.rearrange("(c p) f -> p c f", p=128)
    blk_r = [[] for _ in range(nblk)]
    n = 0
    wn = 0
    on = 0
    for ch in range(4):
        for c in range(NCH):
            s.add("sp", lambda e, c=c, ch=ch: e.dma_start(out=xn[:, c, :], in_=hn_all.ap()[c * 512 + ch * 128:c * 512 + (ch + 1) * 128, :]),
                  reads=[hn_all_r[c]], writes=[xn_r[c]], dma=True)
        for mm in range(nblk // 2):
            slot = wn % NWS
            wn += 1
            s.add("pool", lambda e, mm=mm, slot=slot: e.dma_start(out=ws[:, slot, :, :], in_=w_v[:, :, mm * 256:(mm + 1) * 256]),
                  writes=[ws_r[slot]], dma=True)
            for m2 in range(2):
                m = mm * 2 + m2
                d = dil_of_blk[m]
                os_ = on % NOS
                on += 1
                for t in range(4):
                    cs = slice(t * 512, (t + 1) * 512)
                    pi = n % 6
                    n += 1
                    for c in range(NCH):
                        s.add("pe", lambda e, c=c, slot=slot, m2=m2, cs=cs, pi=pi: e.matmul(
                            cx.ps[pi][:, :], ws[:, slot, c, m2 * 128:(m2 + 1) * 128], xn[:, c, cs],
                            start=(c == 0), stop=(c == NCH - 1)),
                            reads=[ws_r[slot], xn_r[c]], writes=[cx.psr[pi]])
                    if d == 1:
                        o_ap = ost[:, os_, cs]
                        i_ap = cx.ps[pi][:, :]
                    else:
                        w_ = 512 // d
                        o_ap = ost[:, os_, :].rearrange("p (r m) -> p r m", r=d)[:, :, t * w_:(t + 1) * w_].rearrange("p r m -> p m r")
                        i_ap = cx.ps[pi][:, :].rearrange("p (m r) -> p m r", r=d)
                    if bias_ap is not None:
                        s.add("act", lambda e, o_ap=o_ap, i_ap=i_ap, m=m: e.activation(
                            out=o_ap, in_=i_ap, func=AF.Identity, bias=bcol[:, m:m + 1]),
                            reads=[cx.psr[pi], b_r], writes=[ost_r[os_]])
                    elif n % 2 == 0:
                        s.add("act", lambda e, o_ap=o_ap, i_ap=i_ap: e.activation(out=o_ap, in_=i_ap, func=AF.Copy),
                              reads=[cx.psr[pi]], writes=[ost_r[os_]])
                    else:
                        s.add("dve", lambda e, o_ap=o_ap, i_ap=i_ap: e.tensor_copy(out=o_ap, in_=i_ap),
                              reads=[cx.psr[pi]], writes=[ost_r[os_]])
                r_ = Res()
                if d == 1:
                    dst = scratch.ap()[m * 128:(m + 1) * 128, ch * TPC:(ch + 1) * TPC]
                    src = ost[:, os_, :]
                else:
                    dst = scratch.ap()[m * 128:(m + 1) * 128, :].rearrange("p (r c m) -> p r c m", r=d, c=4)[:, :, ch, :]
                    src = ost[:, os_, :].rearrange("p (r m) -> p r m", r=d)
                s.add("sp", lambda e, dst=dst, src=src: e.dma_start(out=dst, in_=src),
                      reads=[ost_r[os_]], writes=[r_], dma=True)
                blk_r[m].append(r_)
    return blk_r


def emit_o_scatter(cx, o_rs_in, o_parts_r, o_my):
    r_ = Res()
    cx.s.add_cc(lambda e: e.collective_compute("ReduceScatter", ALU.add, replica_groups=[[0, 1, 2, 3], [4, 5, 6, 7]],
                                               ins=[o_rs_in.ap()], outs=[o_my.ap()]),
                reads=o_parts_r, writes=[r_])
    return r_


def emit_banded(cx, scratch, blk_r, qblk, kblk, vblk, dils, dneg_ap, cst_ap, sink_ap, ident, ident_r, o_dst, on_piece=None):
    s = cx.s
    cx.phase()
    G = len(dils)
    NB = SEQ // 128
    dneg = cx.ar("dneg", [128, 256], F32)
    dneg_r = Res()
    s.add("sp", lambda e: e.dma_start(out=dneg[:, :], in_=dneg_ap), writes=[dneg_r], dma=True)
    cst = cx.ar("cstb", [128, 2 * G * 2], F32)
    cst_r = Res()
    s.add("sp", lambda e: e.dma_start(out=cst[:, :], in_=cst_ap), writes=[cst_r], dma=True)
    has_sink = sink_ap is not None
    if has_sink:
        sk = cx.ar("sk", [128, 2], F32)
        sk_r = Res()
        s.add("sp", lambda e: e.dma_start(out=sk[:, :], in_=sink_ap), writes=[sk_r], dma=True)
        s.add("act", lambda e: e.activation(out=sk[:, :], in_=sk[:, :], func=AF.Exp), reads=[sk_r], writes=[sk_r])
    ones_b = cx.ar("ones_b", [128, 64], BF16)
    ones_b_r = Res()
    s.add("pool", lambda e: e.memset(ones_b[:, :], 1.0), writes=[ones_b_r])
    NQS = 4
    qs = cx.ar("qs", [128, NQS, 2048], BF16)
    ks = cx.ar("ks", [128, NQS, 2048], BF16)
    vs = cx.ar("vs", [128, NQS, 16, 128], BF16)
    vf = cx.ar("vf", [128, 2, 2048], BF16)
    qs_r = [Res() for _ in range(NQS)]
    ks_r = [Res() for _ in range(NQS)]
    vs_r = [Res() for _ in range(NQS)]
    vf_r = [Res(), Res()]
    accn = cx.ar("accn", [128, SEQ], F32)
    accd = cx.ar("accd", [128, SEQ], F32)
    acc_r = [Res() for _ in range(NB)]
    NP = 4
    pb = cx.ar("pb", [128, NP, 2, 256], BF16)
    pb_r = [[Res(), Res()] for _ in range(NP)]
    NSB = 4
    sbb = cx.ar("sbb", [128, NSB, 256], F32)
    sbb_r = [Res() for _ in range(NSB)]
    ost = cx.ar("ost", [128, 2, 2048], BF16)
    ost_r = [Res(), Res()]
    out_rs = []
    it = 0
    nS = 0
    nPV = 0
    nV = 0
    for pr in range(2):
        for g in range(G):
            d = dils[g]
            nbc = NB // d
            slot_of = lambda Q, it=it: (it * 4 + Q) % NQS
            qb, kb_, vb = qblk[pr][g], kblk[pr][g], vblk[pr][g]
            for Q in range(4):
                sl = slot_of(Q)
                s.add("sp", lambda e, qb=qb, Q=Q, sl=sl: e.dma_start(
                    out=qs[:, sl, :], in_=scratch.ap()[qb * 128:(qb + 1) * 128, Q * 2048:(Q + 1) * 2048]),
                    reads=blk_r[qb], writes=[qs_r[sl]], dma=True)
                s.add("sp", lambda e, kb_=kb_, Q=Q, sl=sl: e.dma_start(
                    out=ks[:, sl, :], in_=scratch.ap()[kb_ * 128:(kb_ + 1) * 128, Q * 2048:(Q + 1) * 2048]),
                    reads=blk_r[kb_], writes=[ks_r[sl]], dma=True)
                vk = nV % 2
                nV += 1
                s.add("sp", lambda e, vb=vb, Q=Q, vk=vk: e.dma_start(
                    out=vf[:, vk, :], in_=scratch.ap()[vb * 128:(vb + 1) * 128, Q * 2048:(Q + 1) * 2048]),
                    reads=blk_r[vb], writes=[vf_r[vk]], dma=True)
                for q4 in range(4):
                    pi = 6 + (q4 % 2)
                    pst = cx.ps[pi][:, :].bitcast(BF16)
                    for j in range(4):
                        lb = q4 * 4 + j
                        s.add("pe", lambda e, pst=pst, j=j, lb=lb, vk=vk: e.transpose(
                            pst[:, j * 128:(j + 1) * 128], vf[:, vk, lb * 128:(lb + 1) * 128], ident[:, :]),
                            reads=[vf_r[vk], ident_r], writes=[cx.psr[pi]])
                    s.add("dve" if q4 % 2 == 0 else "act",
                          (lambda e, pst=pst, sl=sl, q4=q4: e.tensor_copy(out=vs[:, sl, q4 * 4:(q4 + 1) * 4, :], in_=pst[:, 0:512].rearrange("p (j f) -> p j f", j=4)))
                          if q4 % 2 == 0 else
                          (lambda e, pst=pst, sl=sl, q4=q4: e.activation(out=vs[:, sl, q4 * 4:(q4 + 1) * 4, :], in_=pst[:, 0:512].rearrange("p (j f) -> p j f", j=4), func=AF.Copy)),
                          reads=[cx.psr[pi]], writes=[vs_r[sl]])

            def emit_S(B, pr=pr, g=g, d=d, nbc=nbc, slot_of=slot_of):
                nonlocal nS
                b = B % nbc
                last = (b == nbc - 1)
                n = 128 if last else 256
                Q, lb = B // 16, B % 16
                sl = slot_of(Q)
                for e_ in range(2):
                    pi = nS % 4
                    si = nS % NSB
                    nS += 1
                    pslot = B % NP
                    rows = slice(64 * e_, 64 * e_ + 64)
                    s.add("pe", lambda e, pi=pi, sl=sl, lb=lb, rows=rows: e.matmul(
                        cx.ps[pi][:, 0:128], ks[rows, sl, lb * 128:(lb + 1) * 128], qs[rows, sl, lb * 128:(lb + 1) * 128],
                        start=True, stop=True), reads=[ks_r[sl], qs_r[sl]], writes=[cx.psr[pi]])
                    if not last:
                        Q2, lb2 = (B + 1) // 16, (B + 1) % 16
                        sl2 = slot_of(Q2)
                        s.add("pe", lambda e, pi=pi, sl=sl, sl2=sl2, lb=lb, lb2=lb2, rows=rows: e.matmul(
                            cx.ps[pi][:, 128:256], ks[rows, sl, lb * 128:(lb + 1) * 128], qs[rows, sl2, lb2 * 128:(lb2 + 1) * 128],
                            start=True, stop=True), reads=[ks_r[sl], qs_r[sl2]], writes=[cx.psr[pi]])
                    ci = (pr * G + g) * 2 + e_
                    s.add("dve", lambda e, si=si, pi=pi, n=n, ci=ci: e.scalar_tensor_tensor(
                        out=sbb[:, si, 0:n], in0=dneg[:, 0:n], scalar=cst[:, ci:ci + 1], in1=cx.ps[pi][:, 0:n],
                        op0=ALU.mult, op1=ALU.add), reads=[dneg_r, cst_r, cx.psr[pi]], writes=[sbb_r[si]])
                    s.add("act", lambda e, si=si, pslot=pslot, e_=e_, n=n: e.activation(
                        out=pb[:, pslot, e_, 0:n], in_=sbb[:, si, 0:n], func=AF.Exp, scale=0.125),
                        reads=[sbb_r[si]], writes=[pb_r[pslot][e_]])

            def emit_PV(B, pr=pr, g=g, d=d, nbc=nbc, slot_of=slot_of):
                nonlocal nPV
                b = B % nbc
                r = B // nbc
                kcol = 128 * (nPV % 2)
                nPV += 1
                pn, pd_ = 4, 5
                ncol = slice(kcol, kcol + 128)
                srcs = []
                if b > 0:
                    srcs.append((B - 1, slice(128, 256)))
                srcs.append((B, slice(0, 128)))
                for e_ in range(2):
                    orow = slice(64 * e_, 64 * e_ + 64)
                    for i, (kb, cols) in enumerate(srcs):
                        Qk, lbk = kb // 16, kb % 16
                        slk = slot_of(Qk)
                        s.add("pe", lambda e, orow=orow, slk=slk, lbk=lbk, kb=kb, e_=e_, cols=cols, i=i, ncol=ncol: e.matmul(
                            cx.ps[pn][orow, ncol], vs[:, slk, lbk, 64 * e_:64 * e_ + 64], pb[:, kb % NP, e_, cols],
                            start=(i == 0), stop=(i == len(srcs) - 1)),
                            reads=[vs_r[slk], pb_r[kb % NP][e_]], writes=[cx.psr[pn]])
                for e_ in range(2):
                    orow = slice(64 * e_, 64 * e_ + 64)
                    for i, (kb, cols) in enumerate(srcs):
                        s.add("pe", lambda e, orow=orow, kb=kb, e_=e_, cols=cols, i=i, ncol=ncol: e.matmul(
                            cx.ps[pd_][orow, ncol], ones_b[:, :], pb[:, kb % NP, e_, cols],
                            start=(i == 0), stop=(i == len(srcs) - 1)),
                            reads=[ones_b_r, pb_r[kb % NP][e_]], writes=[cx.psr[pd_]])
                off = b * 128 * d + r
                view = slice(off, off + 127 * d + 1, d) if d > 1 else slice(off, off + 128)
                ar = [acc_r[j] for j in range(b * d, (b + 1) * d)]
                if g == 0:
                    s.add("act", lambda e, view=view, ncol=ncol: e.activation(out=accn[:, view], in_=cx.ps[pn][:, ncol], func=AF.Copy),
                          reads=[cx.psr[pn]], writes=ar)
                    if has_sink:
                        s.add("dve", lambda e, view=view, pr=pr, ncol=ncol: e.tensor_scalar(
                            out=accd[:, view], in0=cx.ps[pd_][:, ncol], scalar1=sk[:, pr:pr + 1], scalar2=None, op0=ALU.add),
                            reads=[cx.psr[pd_], sk_r], writes=ar)
                    else:
                        s.add("dve", lambda e, view=view, ncol=ncol: e.tensor_copy(out=accd[:, view], in_=cx.ps[pd_][:, ncol]),
                              reads=[cx.psr[pd_]], writes=ar)
                else:
                    s.add("dve", lambda e, view=view, ncol=ncol: e.tensor_tensor(
                        out=accn[:, view], in0=accn[:, view], in1=cx.ps[pn][:, ncol], op=ALU.add),
                        reads=[cx.psr[pn]], writes=ar)
                    s.add("dve", lambda e, view=view, ncol=ncol: e.tensor_tensor(
                        out=accd[:, view], in0=accd[:, view], in1=cx.ps[pd_][:, ncol], op=ALU.add),
                        reads=[cx.psr[pd_]], writes=ar)

            emit_S(0)
            for B in range(NB):
                if B + 1 < NB:
                    emit_S(B + 1)
                emit_PV(B)
            it += 1
        for Q in range(4):
            cs = slice(Q * 2048, (Q + 1) * 2048)
            ar = [acc_r[j] for j in range(Q * 16, (Q + 1) * 16)]
            k = Q % 2
            s.add("dve", lambda e, cs=cs: e.reciprocal(out=accd[:, cs], in_=accd[:, cs]), reads=ar, writes=ar)
            s.add("pool", lambda e, cs=cs, k=k: e.tensor_tensor(out=ost[:, k, :], in0=accn[:, cs], in1=accd[:, cs], op=ALU.mult),
                  reads=ar, writes=[ost_r[k]])
            r_ = Res()
            s.add("pool", lambda e, pr=pr, Q=Q, k=k: e.dma_start(out=o_dst(pr, Q), in_=ost[:, k, :]),
                  reads=[ost_r[k]], writes=[r_], dma=True)
            out_rs.append(r_)
            if on_piece is not None:
                on_piece(pr, Q, [r_])
    return out_rs


def emit_diff(cx, scratch, blk_r, lambda_init, mt_ap, cst_ap, lam_ap, sub_ap, ident, ident_r, o_dst, on_piece=None):
    s = cx.s
    cx.phase()
    NB = SEQ // 128
    NQT = SEQ // 512
    mtab = cx.ar("mtab", [128, 5, 512], F32)
    mtab_r = Res()
    s.add("sp", lambda e: e.dma_start(out=mtab[:, :, :], in_=mt_ap), writes=[mtab_r], dma=True)
    cst = cx.ar("cstd", [128, 2 + 128], F32)
    cst_r = Res()
    s.add("sp", lambda e: e.dma_start(out=cst[:, :], in_=cst_ap), writes=[cst_r], dma=True)
    lam = cx.ar("lam", [128, 4, 64], F32)
    lam_r = Res()
    s.add("sp", lambda e: e.dma_start(out=lam[:, :, :], in_=lam_ap), writes=[lam_r], dma=True)
    sub = cx.ar("sub", [128, 1], F32)
    sub_r = Res()
    s.add("sp", lambda e: e.dma_start(out=sub[:, :], in_=sub_ap), writes=[sub_r], dma=True)
    ones_b = cx.ar("ones_b", [128, 128], BF16)
    ones_b_r = Res()
    s.add("pool", lambda e: e.memset(ones_b[:, :], 1.0), writes=[ones_b_r])
    lp = cx.ar("lp", [128, 2, 64], F32)
    lsum = cx.ar("lsum", [128, 2], F32)
    neglam = cx.ar("neglam", [128, 1], F32)
    lp_r, lsum_r, neglam_r = Res(), Res(), Res()
    for i in range(2):
        s.add("dve", lambda e, i=i: e.tensor_tensor(out=lp[:, i, :], in0=lam[:, 2 * i, :], in1=lam[:, 2 * i + 1, :], op=ALU.mult),
              reads=[lam_r], writes=[lp_r])
        s.add("dve", lambda e, i=i: e.reduce_sum(out=lsum[:, i:i + 1], in_=lp[:, i, :], axis=mybir.AxisListType.X),
              reads=[lp_r], writes=[lsum_r])
    s.add("act", lambda e: e.activation(out=lsum[:, :], in_=lsum[:, :], func=AF.Exp), reads=[lsum_r], writes=[lsum_r])
    s.add("dve", lambda e: e.tensor_tensor(out=neglam[:, :], in0=lsum[:, 1:2], in1=lsum[:, 0:1], op=ALU.subtract),
          reads=[lsum_r], writes=[neglam_r])
    s.add("dve", lambda e: e.tensor_scalar(out=neglam[:, :], in0=neglam[:, :], scalar1=-float(lambda_init), scalar2=None, op0=ALU.add),
          reads=[neglam_r], writes=[neglam_r])
    qs = cx.ar("qs", [128, SEQ], BF16)
    ks = cx.ar("ks", [128, SEQ], BF16)
    vs = cx.ar("vs", [128, NB, 128], BF16)
    vf = cx.ar("vf", [128, 2, 2048], BF16)
    qs_r = [Res() for _ in range(4)]
    ks_r = [Res() for _ in range(4)]
    vs_r = [Res() for _ in range(4)]
    vf_r = [Res(), Res()]
    NP = 6
    pb = cx.ar("pb", [128, NP, 512], BF16)
    pb_r = [Res() for _ in range(NP)]
    NSB = 4
    sbb = cx.ar("sbb", [128, NSB, 512], F32)
    sbb_r = [Res() for _ in range(NSB)]
    rd = cx.ar("rd", [128, 2, 512], F32)
    rd_r = [Res(), Res()]
    tt_ = cx.ar("tt", [128, 2, 512], F32)
    tt_r = [Res(), Res()]
    ob = cx.ar("ob", [128, 512], F32)
    ob_r = Res()
    sq = cx.ar("sq", [128, 2, 512], F32)
    sq_r = [Res(), Res()]
    rstd = cx.ar("rstd", [128, 512], F32)
    rstd_r = Res()
    ost = cx.ar("ost", [128, 2, 512], BF16)
    ost_r = [Res(), Res()]
    out_rs = []
    nS = 0
    nV = 0
    for i in range(2):
        qb, kb_, vb = 3 * i, 3 * i + 1, 3 * i + 2
        for Q in range(4):
            s.add("sp", lambda e, qb=qb, Q=Q: e.dma_start(out=qs[:, Q * 2048:(Q + 1) * 2048], in_=scratch.ap()[qb * 128:(qb + 1) * 128, Q * 2048:(Q + 1) * 2048]),
                  reads=blk_r[qb], writes=[qs_r[Q]], dma=True)
            s.add("sp", lambda e, kb_=kb_, Q=Q: e.dma_start(out=ks[:, Q * 2048:(Q + 1) * 2048], in_=scratch.ap()[kb_ * 128:(kb_ + 1) * 128, Q * 2048:(Q + 1) * 2048]),
                  reads=blk_r[kb_], writes=[ks_r[Q]], dma=True)
            vk = nV % 2
            nV += 1
            s.add("sp", lambda e, vb=vb, Q=Q, vk=vk: e.dma_start(out=vf[:, vk, :], in_=scratch.ap()[vb * 128:(vb + 1) * 128, Q * 2048:(Q + 1) * 2048]),
                  reads=blk_r[vb], writes=[vf_r[vk]], dma=True)
            for q4 in range(4):
                pi = q4 % 4
                pst = cx.ps[pi][:, :].bitcast(BF16)
                for j in range(4):
                    lb = q4 * 4 + j
                    s.add("pe", lambda e, pst=pst, j=j, lb=lb, vk=vk: e.transpose(
                        pst[:, j * 128:(j + 1) * 128], vf[:, vk, lb * 128:(lb + 1) * 128], ident[:, :]),
                        reads=[vf_r[vk], ident_r], writes=[cx.psr[pi]])
                b0 = Q * 16 + q4 * 4
                s.add("dve" if q4 % 2 == 0 else "act",
                      (lambda e, pst=pst, b0=b0: e.tensor_copy(out=vs[:, b0:b0 + 4, :], in_=pst[:, 0:512].rearrange("p (j f) -> p j f", j=4)))
                      if q4 % 2 == 0 else
                      (lambda e, pst=pst, b0=b0: e.activation(out=vs[:, b0:b0 + 4, :], in_=pst[:, 0:512].rearrange("p (j f) -> p j f", j=4), func=AF.Copy)),
                      reads=[cx.psr[pi]], writes=[vs_r[Q]])
        for qt in range(NQT):
            Qq = qt // 4
            qc0 = qt * 512
            kblocks = [(kb, None) for kb in range(4 * qt)] + [(4 * qt + j, j) for j in range(4)]
            nk = len(kblocks)

            def emit_S(idx, i=i, qt=qt, Qq=Qq, qc0=qc0, kblocks=kblocks):
                nonlocal nS
                kb, dj = kblocks[idx]
                Qk = kb // 16
                c0 = 0 if dj is None else 128 * dj
                mi = 0 if dj is None else 1 + dj
                nrel = (4 * qt - kb) if dj is None else 0
                res = []
                for c in range(2):
                    pi = nS % 4
                    si = nS % NSB
                    pslot = nS % NP
                    nS += 1
                    rows = slice(64 * c, 64 * c + 64)
                    s.add("pe", lambda e, pi=pi, rows=rows, kb=kb, c0=c0: e.matmul(
                        cx.ps[pi][:, c0:512], ks[rows, kb * 128:(kb + 1) * 128], qs[rows, qc0 + c0:qc0 + 512],
                        start=True, stop=True), reads=[ks_r[Qk], qs_r[Qq]], writes=[cx.psr[pi]])
                    s.add("dve", lambda e, si=si, pi=pi, mi=mi, c0=c0: e.scalar_tensor_tensor(
                        out=sbb[:, si, c0:512], in0=mtab[:, mi, c0:512], scalar=cst[:, i:i + 1], in1=cx.ps[pi][:, c0:512],
                        op0=ALU.mult, op1=ALU.add), reads=[mtab_r, cst_r, cx.psr[pi]], writes=[sbb_r[si]])
                    bc = 2 + i * 64 + nrel
                    s.add("act", lambda e, si=si, pslot=pslot, c0=c0, bc=bc: e.activation(
                        out=pb[:, pslot, c0:512], in_=sbb[:, si, c0:512], func=AF.Exp, scale=0.125, bias=cst[:, bc:bc + 1]),
                        reads=[sbb_r[si], cst_r], writes=[pb_r[pslot]])
                    res.append((pslot, c0))
                return res

            def emit_PV(idx, pinfo, i=i, kblocks=kblocks, nk=nk):
                kb, dj = kblocks[idx]
                Qk = kb // 16
                for c in range(2):
                    pslot, c0 = pinfo[c]
                    s.add("pe", lambda e, c=c, kb=kb, pslot=pslot, c0=c0, idx=idx: e.matmul(
                        cx.ps[4 + c][:, c0:512], vs[:, kb, :], pb[:, pslot, c0:512],
                        start=(idx == 0), stop=(idx == nk - 1)),
                        reads=[vs_r[Qk], pb_r[pslot]], writes=[cx.psr[4 + c]])
                    s.add("pe", lambda e, c=c, pslot=pslot, c0=c0, idx=idx: e.matmul(
                        cx.ps[6 + c][:, c0:512], ones_b[:, :], pb[:, pslot, c0:512],
                        start=(idx == 0), stop=(idx == nk - 1)),
                        reads=[ones_b_r, pb_r[pslot]], writes=[cx.psr[6 + c]])

            pin = emit_S(0)
            for idx in range(nk):
                nxt = emit_S(idx + 1) if idx + 1 < nk else None
                emit_PV(idx, pin)
                pin = nxt
            for c in range(2):
                s.add("dve", lambda e, c=c: e.reciprocal(out=rd[:, c, :], in_=cx.ps[6 + c][:, :]),
                      reads=[cx.psr[6 + c]], writes=[rd_r[c]])
                s.add("dve", lambda e, c=c: e.tensor_tensor(out=tt_[:, c, :], in0=cx.ps[4 + c][:, :], in1=rd[:, c, :], op=ALU.mult),
                      reads=[cx.psr[4 + c], rd_r[c]], writes=[tt_r[c]])
            s.add("dve", lambda e: e.scalar_tensor_tensor(out=ob[:, :], in0=tt_[:, 1, :], scalar=neglam[:, 0:1], in1=tt_[:, 0, :],
                                                          op0=ALU.mult, op1=ALU.add),
                  reads=[tt_r[0], tt_r[1], neglam_r], writes=[ob_r])
            emit_rstd(cx, lambda c: ob[:, :], 1, 512, rstd, rstd_r, lambda c: [ob_r], sq, sq_r, (nS % 4), inv_n=1.0 / 128)
            s.add("dve", lambda e: e.scalar_tensor_tensor(out=ob[:, :], in0=ob[:, :], scalar=sub[:, 0:1], in1=rstd[:, :],
                                                          op0=ALU.mult, op1=ALU.mult),
                  reads=[ob_r, sub_r, rstd_r], writes=[ob_r])
            k_ = qt % 2
            s.add("act", lambda e, k_=k_: e.activation(out=ost[:, k_, :], in_=ob[:, :], func=AF.Copy, scale=float(1.0 - lambda_init)),
                  reads=[ob_r], writes=[ost_r[k_]])
            r_ = Res()
            s.add("pool", lambda e, i=i, qt=qt, k_=k_: e.dma_start(out=o_dst(i, qt), in_=ost[:, k_, :]),
                  reads=[ost_r[k_]], writes=[r_], dma=True)
            out_rs.append(r_)
            if on_piece is not None and qt % 4 == 3:
                on_piece(i, qt // 4, out_rs[-4:])
    return out_rs


def emit_outproj(cx, x, x_r, o_my, o_my_r, w_ap, gcol, g_r):
    s = cx.s
    cx.phase()
    T = TPC
    NT = T // 512
    w = cx.ar("wo", [128, NCH, D], BF16)
    w_r = Res()
    s.add("pool", lambda e: e.dma_start(out=w[:, :, :], in_=w_ap.rearrange("(c p) f -> p c f", p=128)),
          writes=[w_r], dma=True)
    o = cx.ar("oo", [128, NCH, T], BF16)
    o_r = [Res() for _ in range(NCH)]
    for c in range(NCH):
        s.add("sp", lambda e, c=c: e.dma_start(out=o[:, c, :], in_=o_my.ap()[c * 128:(c + 1) * 128, :]),
              reads=[o_my_r], writes=[o_r[c]], dma=True)
    fT = cx.ar("fTo", [128, NCH, 512], F32)
    fT_r = [Res() for _ in range(NCH)]
    sq = cx.ar("sq", [128, 2, 512], F32)
    sq_r = [Res(), Res()]
    rstd = cx.ar("rstd", [128, 512], F32)
    rstd_r = Res()
    tmp = cx.ar("tmp", [128, 2, 512], F32)
    tmp_r = [Res(), Res()]
    n = 0
    for t in range(NT):
        cs = slice(t * 512, (t + 1) * 512)
        for c in range(NCH):
            pi = n % 4
            n += 1
            for k in range(NCH):
                s.add("pe", lambda e, c=c, k=k, cs=cs, pi=pi: e.matmul(
                    cx.ps[pi][:, :], w[:, k, c * 128:(c + 1) * 128], o[:, k, cs],
                    start=(k == 0), stop=(k == NCH - 1)),
                    reads=[w_r, o_r[k]], writes=[cx.psr[pi]])
            s.add("act", lambda e, c=c, pi=pi: e.activation(out=fT[:, c, :], in_=cx.ps[pi][:, :], func=AF.Copy),
                  reads=[cx.psr[pi]], writes=[fT_r[c]])
        emit_rstd(cx, lambda c: fT[:, c, :], NCH, 512, rstd, rstd_r, lambda c: [fT_r[c]], sq, sq_r, 6 + (t % 2))
        for c in range(NCH):
            k = c % 2
            s.add("dve", lambda e, c=c, k=k: e.scalar_tensor_tensor(
                out=tmp[:, k, :], in0=fT[:, c, :], scalar=gcol(c), in1=rstd[:, :],
                op0=ALU.mult, op1=ALU.mult),
                reads=[fT_r[c], g_r, rstd_r], writes=[tmp_r[k]])
            s.add("pool", lambda e, c=c, cs=cs, k=k: e.tensor_tensor(
                out=x[:, c, cs], in0=x[:, c, cs], in1=tmp[:, k, :], op=ALU.add),
                reads=[tmp_r[k]], writes=[x_r[c][t]])


GROUPS4 = [[0, 1, 2, 3], [4, 5, 6, 7]]
LAYER_KINDS = (0, 1, 2, 0)
A_DILS = (1, 4, 16)


def emit_gather_pieces(cx, snd, snd_rs, rcv, npieces):
    outs = []
    for i in range(npieces):
        r_ = Res()
        cx.s.add_cc(lambda e, i=i: e.collective_compute(
            "AllGather", ALU.bypass, replica_groups=GROUPS4,
            ins=[snd.ap()[i * 128:(i + 1) * 128, :]], outs=[rcv.ap()[i * 512:(i + 1) * 512, :]]),
            reads=snd_rs[i], writes=[r_])
        outs.append(r_)
    return outs


def emit_hn_pieces(cx, x, x_r, gc, g_r, hn_snd):
    s_ = cx.s
    NT = TPC // 512
    cx.phase()
    sq = cx.ar("sq", [128, 2, 512], F32)
    rstd = cx.ar("rstd", [128, 512], F32)
    hst = cx.ar("hst", [128, 2, NCH, 512], BF16)
    sq_r, rstd_r = [Res(), Res()], Res()
    hst_r = [[Res() for _ in range(NCH)] for _ in range(2)]
    snd_rs = [[] for _ in range(NCH)]
    for t in range(NT):
        k = t % 2
        cs = slice(t * 512, (t + 1) * 512)
        emit_rstd(cx, lambda c, cs=cs: x[:, c, cs], NCH, 512, rstd, rstd_r,
                  lambda c, t=t: [x_r[c][t]], sq, sq_r, 6 + k)
        for c in range(NCH):
            s_.add("dve", lambda e, c=c, cs=cs, k=k: e.scalar_tensor_tensor(
                out=hst[:, k, c, :], in0=x[:, c, cs], scalar=gc(c), in1=rstd[:, :],
                op0=ALU.mult, op1=ALU.mult),
                reads=[x_r[c][t], g_r, rstd_r], writes=[hst_r[k][c]])
            r_ = Res()
            s_.add("sp", lambda e, c=c, cs=cs, k=k: e.dma_start(out=hn_snd.ap()[c * 128:(c + 1) * 128, cs], in_=hst[:, k, c, :]),
                   reads=[hst_r[k][c]], writes=[r_], dma=True)
            snd_rs[c].append(r_)
    return snd_rs


def build_fused(nlayers=4, layers=None, dbg=False):
    cx = Ctx()
    nc, s = cx.nc, cx.s
    T = TPC
    NT = T // 512
    xT_d = cx.dram_in("xT", [D, T])
    gpre_d = cx.dram_in("gpre", [128, 96])
    gpost_d = cx.dram_in("gpost", [128, 96])
    wg_d = cx.dram_in("ffn_wg", [4, 2, D, DFF])
    wu_d = cx.dram_in("ffn_wu", [4, 2, D, DFF])
    wd_d = cx.dram_in("ffn_wd", [4, 2, DFF, D])
    a_win_d = cx.dram_in("a_win", [2, D, 18 * 128])
    a_wout_d = cx.dram_in("a_wout", [2, D, D])
    b_win_d = cx.dram_in("b_win", [D, 6 * 128])
    b_wout_d = cx.dram_in("b_wout", [D, D])
    c_win_d = cx.dram_in("c_win", [D, 4 * 128])
    c_b_d = cx.dram_in("c_b", [128, 4])
    c_wout_d = cx.dram_in("c_wout", [D, D])
    dnegA_d = cx.dram_in("dnegA", [128, 256])
    dnegC_d = cx.dram_in("dnegC", [128, 256])
    cstA_d = cx.dram_in("cstA", [128, 12])
    cstC_d = cx.dram_in("cstC", [128, 4])
    sinkC_d = cx.dram_in("sinkC", [128, 2])
    mtab_d = cx.dram_in("mtab", [128, 5, 512])
    cstB_d = cx.dram_in("cstB", [128, 130])
    lam_d = cx.dram_in("lam", [128, 4, 64])
    sub_d = cx.dram_in("sub", [128, 1])
    ident_d = cx.dram_in("ident", [128, 128], BF16)
    out_d = cx.dram_out("yT", [D, T])
    hn_snd = cx.dram("hn_snd", [NCH * 128, T], BF16)
    hn_all = cx.dram("hn_all", [NCH * 512, T], BF16)
    scratch = cx.dram("scratch", [18 * 128, SEQ], BF16)
    o_snd = cx.dram("o_snd", [8 * 128, T], BF16)
    o_all = cx.dram("o_all", [8 * 512, T], BF16)
    x = cx.perm("x", [128, NCH, T], F32)
    x_r = [[Res() for _ in range(NT)] for _ in range(NCH)]
    gpre = cx.perm("gpre_s", [128, 96], F32)
    gpost = cx.perm("gpost_s", [128, 96], F32)
    gposth = cx.perm("gposth_s", [128, 96], F32)
    ident = cx.perm("ident_s", [128, 128], BF16)
    g_r, ident_r = Res(), Res()
    cx.arena_start()
    s.add("sp", lambda e: e.dma_start(out=gpre[:, :], in_=gpre_d.ap()), writes=[g_r], dma=True)
    s.add("sp", lambda e: e.dma_start(out=gpost[:, :], in_=gpost_d.ap()), writes=[g_r], dma=True)
    s.add("sp", lambda e: e.dma_start(out=ident[:, :], in_=ident_d.ap()), writes=[ident_r], dma=True)
    s.add("dve", lambda e: e.tensor_scalar(out=gposth[:, :], in0=gpost[:, :], scalar1=0.5, scalar2=None, op0=ALU.mult),
          reads=[g_r], writes=[g_r])
    for c in range(NCH):
        for t in range(NT):
            s.add("sp", lambda e, c=c, t=t: e.dma_start(out=x[:, c, t * 512:(t + 1) * 512],
                                                        in_=xT_d.ap()[c * 128:(c + 1) * 128, t * 512:(t + 1) * 512]),
                  writes=[x_r[c][t]], dma=True)

    def gcol(tab, i, sl):
        return lambda c: tab[:, (i * 3 + sl) * 8 + c:(i * 3 + sl) * 8 + c + 1]

    for i in (layers if layers is not None else range(nlayers)):
        kind, j = LAYER_KINDS[i], i // 3
        cx.phase()
        cx.s.new_epoch()
        emit_ffn(cx, x, x_r, wg_d.ap()[i, 0], wu_d.ap()[i, 0], wd_d.ap()[i, 0], gcol(gpre, i, 0), gcol(gposth, i, 0), g_r)
        if dbg:
            dump_x(cx, x, x_r, "dbg%d_a" % i)
        snd_rs = emit_hn_pieces(cx, x, x_r, gcol(gpre, i, 1), g_r, hn_snd)
        hn_rs = emit_gather_pieces(cx, hn_snd, snd_rs, hn_all, NCH)
        if kind == 0:
            nblk, w_ap, bias_ap = 18, a_win_d.ap()[j], None
            dil_of_blk = [A_DILS[(b_ // 3) % 3] for b_ in range(18)]
        elif kind == 1:
            nblk, w_ap, bias_ap = 6, b_win_d.ap(), None
            dil_of_blk = [1] * 6
        else:
            nblk, w_ap, bias_ap = 4, c_win_d.ap(), c_b_d.ap()
            dil_of_blk = [1] * 4
        blk_r = emit_proj_heads(cx, hn_all, hn_rs, w_ap, nblk, dil_of_blk, scratch, bias_ap)
        o_dst = lambda pr, Q: o_snd.ap()[(Q * 2 + pr) * 128:(Q * 2 + pr + 1) * 128, :]
        o_rs = []

        def on_piece(pr, Q, rs, o_rs=o_rs):
            pi_ = Q * 2 + pr
            r_ = Res()
            cx.s.add_cc(lambda e, pi_=pi_: e.collective_compute(
                "AllGather", ALU.bypass, replica_groups=GROUPS4,
                ins=[o_snd.ap()[pi_ * 128:(pi_ + 1) * 128, :]], outs=[o_all.ap()[pi_ * 512:(pi_ + 1) * 512, :]]),
                reads=rs, writes=[r_])
            o_rs.append(r_)
        if kind == 0:
            qblk = [[(pr * 3 + g) * 3 + 0 for g in range(3)] for pr in range(2)]
            kblk = [[(pr * 3 + g) * 3 + 1 for g in range(3)] for pr in range(2)]
            vblk = [[(pr * 3 + g) * 3 + 2 for g in range(3)] for pr in range(2)]
            emit_banded(cx, scratch, blk_r, qblk, kblk, vblk, A_DILS, dnegA_d.ap(), cstA_d.ap(), None, ident, ident_r, o_dst, on_piece)
        elif kind == 2:
            emit_banded(cx, scratch, blk_r, [[0], [1]], [[2], [2]], [[3], [3]], (1,), dnegC_d.ap(), cstC_d.ap(), sinkC_d.ap(),
                        ident, ident_r, o_dst, on_piece)
        else:
            li = 0.8 - 0.6 * math.exp(-0.3 * i)
            o_dst_b = lambda ih, qt: o_snd.ap()[((qt // 4) * 2 + ih) * 128:((qt // 4) * 2 + ih + 1) * 128, (qt % 4) * 512:(qt % 4 + 1) * 512]
            emit_diff(cx, scratch, blk_r, li, mtab_d.ap(), cstB_d.ap(), lam_d.ap(), sub_d.ap(), ident, ident_r, o_dst_b, on_piece)
        w_out_ap = (a_wout_d.ap()[j] if kind == 0 else (b_wout_d.ap() if kind == 1 else c_wout_d.ap()))
        emit_outproj_dyn(cx, x, x_r, o_all, o_rs, w_out_ap, gcol(gpost, i, 1), g_r)
        if dbg:
            dump_x(cx, x, x_r, "dbg%d_b" % i)
        emit_ffn(cx, x, x_r, wg_d.ap()[i, 1], wu_d.ap()[i, 1], wd_d.ap()[i, 1], gcol(gpre, i, 2), gcol(gposth, i, 2), g_r)
    cx.phase()
    for c in range(NCH):
        for t in range(NT):
            s.add("sp", lambda e, c=c, t=t: e.dma_start(out=out_d.ap()[c * 128:(c + 1) * 128, t * 512:(t + 1) * 512],
                                                        in_=x[:, c, t * 512:(t + 1) * 512]),
                  reads=[x_r[c][t]], dma=True)
    s.emit()
    return nc


def dump_x(cx, x, x_r, name):
    cx.phase()
    d = cx.dram_out(name, [D, TPC])
    for c in range(NCH):
        for t in range(TPC // 512):
            cx.s.add("sp", lambda e, c=c, t=t: e.dma_start(out=d.ap()[c * 128:(c + 1) * 128, t * 512:(t + 1) * 512],
                                                           in_=x[:, c, t * 512:(t + 1) * 512]),
                     reads=[x_r[c][t]], dma=True)


def emit_outproj_dyn(cx, x, x_r, o_all, o_rs, w_ap, gcol, g_r):
    s = cx.s
    s.want_pid = True
    cx.phase()
    T = TPC
    NT = T // 512
    w = cx.ar("wo", [128, NCH, D], BF16)
    w_r = Res()
    s.add("pool", lambda e: e.dma_start(out=w[:, :, :], in_=w_ap.rearrange("(c p) f -> p c f", p=128)),
          writes=[w_r], dma=True)
    o = cx.ar("oo", [128, NCH, T], BF16)
    o_r = [Res() for _ in range(NCH)]
    for hq in range(4):
        for pr in range(2):
            c = hq * 2 + pr

            def fn(e, hq=hq, pr=pr, c=c):
                rank = cx.s.pid % 4
                return e.dma_start(out=o[:, c, :], in_=o_all.ap()[bass.ds(rank * 1024 + pr * 512 + hq * 128, 128), :])
            s.add("pool", fn, reads=o_rs, writes=[o_r[c]], dma=True)
    fT = cx.ar("fTo", [128, NCH, 512], F32)
    fT_r = [Res() for _ in range(NCH)]
    sq = cx.ar("sq", [128, 2, 512], F32)
    sq_r = [Res(), Res()]
    rstd = cx.ar("rstd", [128, 512], F32)
    rstd_r = Res()
    tmp = cx.ar("tmp", [128, 2, 512], F32)
    tmp_r = [Res(), Res()]
    n = 0
    for t in range(NT):
        cs = slice(t * 512, (t + 1) * 512)
        for c in range(NCH):
            pi = n % 4
            n += 1
            for k in range(NCH):
                s.add("pe", lambda e, c=c, k=k, cs=cs, pi=pi: e.matmul(
                    cx.ps[pi][:, :], w[:, k, c * 128:(c + 1) * 128], o[:, k, cs],
                    start=(k == 0), stop=(k == NCH - 1)),
                    reads=[w_r, o_r[k]], writes=[cx.psr[pi]])
            s.add("act", lambda e, c=c, pi=pi: e.activation(out=fT[:, c, :], in_=cx.ps[pi][:, :], func=AF.Copy),
                  reads=[cx.psr[pi]], writes=[fT_r[c]])
        emit_rstd(cx, lambda c: fT[:, c, :], NCH, 512, rstd, rstd_r, lambda c: [fT_r[c]], sq, sq_r, 6 + (t % 2))
        for c in range(NCH):
            k = c % 2
            s.add("dve", lambda e, c=c, k=k: e.scalar_tensor_tensor(
                out=tmp[:, k, :], in0=fT[:, c, :], scalar=gcol(c), in1=rstd[:, :],
                op0=ALU.mult, op1=ALU.mult),
                reads=[fT_r[c], g_r, rstd_r], writes=[tmp_r[k]])
            s.add("pool", lambda e, c=c, cs=cs, k=k: e.tensor_tensor(
                out=x[:, c, cs], in0=x[:, c, cs], in1=tmp[:, k, :], op=ALU.add),
                reads=[tmp_r[k]], writes=[x_r[c][t]])


def alibi_slopes(n):
    return np.array([2.0 ** (-8.0 * (h + 1) / n) for h in range(n)], dtype=np.float64)


MASKV = -1.0e30


def dneg_table(max_dist):
    k = np.arange(128)[:, None]
    q = np.arange(128)[None, :]
    cur = np.where(q - k >= 0, -(q - k).astype(np.float64), MASKV)
    dn = q + 128 - k
    nxt = np.where(dn <= max_dist, -dn.astype(np.float64), MASKV)
    return np.ascontiguousarray(np.concatenate([cur, nxt], axis=1).astype(np.float32))


def diff_mtab():
    k = np.arange(128)[:, None].astype(np.float64)
    q = np.arange(512)[None, :].astype(np.float64)
    tabs = [-(q - k)]
    for j in range(4):
        val = q - 128 * j - k
        tabs.append(np.where(val >= 0, -val, MASKV))
    return np.ascontiguousarray(np.stack(tabs, axis=1).astype(np.float32))


def _f32(a):
    return np.ascontiguousarray(np.asarray(a, dtype=np.float32))


def _gains(g):
    g = np.asarray(g, dtype=np.float32)
    return np.ascontiguousarray(g.reshape(4, 3, 8, 128).transpose(3, 0, 1, 2).reshape(128, 96))


_NC_CACHE = {}


def make_in_maps(x, norm_pre, norm_post, ffn_w_gate, ffn_w_up, ffn_w_down, a_w_in, a_w_out, b_w_in, b_w_out,
                 b_lambda, b_subln, c_w_in, c_b_in, c_w_out, c_sinks):
    x = np.asarray(x, dtype=np.float32)
    a_w_in = np.asarray(a_w_in, dtype=np.float32)
    b_w_in = np.asarray(b_w_in, dtype=np.float32)
    c_w_in = np.asarray(c_w_in, dtype=np.float32)
    c_b_in = np.asarray(c_b_in, dtype=np.float32)
    sinks = np.asarray(c_sinks, dtype=np.float32)[0]
    common = dict(
        gpre=_gains(norm_pre), gpost=_gains(norm_post),
        ffn_wg=_f32(ffn_w_gate), ffn_wu=_f32(ffn_w_up), ffn_wd=_f32(ffn_w_down),
        a_wout=_f32(a_w_out), b_wout=_f32(np.asarray(b_w_out)[0]), c_wout=_f32(np.asarray(c_w_out)[0]),
        dnegA=dneg_table(128), dnegC=dneg_table(127), mtab=diff_mtab(),
        lam=np.ascontiguousarray(np.broadcast_to(np.asarray(b_lambda, dtype=np.float32)[0][None], (128, 4, 64))),
        sub=_f32(np.asarray(b_subln)[0]).reshape(128, 1),
        ident=np.eye(128, dtype=np.float32).astype(NPBF16),
    )
    sl16, sl8 = alibi_slopes(16), alibi_slopes(8)
    maps = []
    for c in range(NCORES):
        b_, hq = c // 4, c % 4
        m = dict(common)
        m["xT"] = np.ascontiguousarray(x[b_, hq * TPC:(hq + 1) * TPC, :].T)
        cols = []
        cstA = np.zeros((128, 12), np.float32)
        for pr in range(2):
            h0 = 4 * hq + 2 * pr
            for g in range(3):
                for cc in range(3):
                    base = ((g * 3 + cc) * 16 + h0) * 64
                    cols.append(np.arange(base, base + 128))
                for e in range(2):
                    cstA[:, (pr * 3 + g) * 2 + e] = 8.0 * sl16[h0 + e] * A_DILS[g]
        cols = np.concatenate(cols)
        m["a_win"] = np.ascontiguousarray(a_w_in[:, :, cols])
        m["cstA"] = cstA
        cols = []
        cstB = np.zeros((128, 130), np.float32)
        for i in range(2):
            h = 2 * hq + i
            for cc in range(3):
                cols.append(np.arange(cc * 1024 + h * 128, cc * 1024 + (h + 1) * 128))
            cstB[:, i] = 8.0 * sl8[h]
            cstB[:, 2 + i * 64:2 + (i + 1) * 64] = (-sl8[h] * 128.0 * np.arange(64))[None, :]
        cols = np.concatenate(cols)
        m["b_win"] = np.ascontiguousarray(b_w_in[0][:, cols])
        m["cstB"] = cstB
        kv = hq // 2
        cols = np.concatenate([np.arange(4 * hq * 64, 4 * hq * 64 + 256),
                               np.arange(1024 + kv * 64, 1024 + kv * 64 + 64), np.arange(1024 + kv * 64, 1024 + kv * 64 + 64),
                               np.arange(1152 + kv * 64, 1152 + kv * 64 + 64), np.arange(1152 + kv * 64, 1152 + kv * 64 + 64)])
        m["c_win"] = np.ascontiguousarray(c_w_in[0][:, cols])
        m["c_b"] = np.ascontiguousarray(c_b_in[0][cols].reshape(4, 128).T)
        cstC = np.zeros((128, 4), np.float32)
        sk = np.zeros((128, 2), np.float32)
        for pr in range(2):
            h0 = 4 * hq + 2 * pr
            for e in range(2):
                cstC[:, pr * 2 + e] = 8.0 * sl16[h0 + e]
                sk[64 * e:64 * e + 64, pr] = sinks[h0 + e]
        m["cstC"] = cstC
        m["sinkC"] = sk
        maps.append(m)
    return maps


def kernel(**inputs):
    if "nc" not in _NC_CACHE:
        _NC_CACHE["nc"] = build_fused(4)
    maps = make_in_maps(**inputs)
    res = run_bass_kernel_spmd(_NC_CACHE["nc"], maps, core_ids=list(range(NCORES)))
    out = np.empty((BATCH, SEQ, D), np.float32)
    for c in range(NCORES):
        b_, ch = c // 4, c % 4
        out[b_, ch * TPC:(ch + 1) * TPC, :] = np.asarray(res.results[c]["yT"]).T
    return out
```
